# Optimizing a Trainium2 kernel written in Bass

```python
import jax, jax.numpy as jnp
from jax import lax
import numpy as np

D_MODEL = 1024
BATCH = 8
SEQ = 2048
DEPTH = 4
DEC_BATCH = 16
DEC_SEQ = 32
PAST_LEN = 4096

CHUNK = 64
N_META = 16
HA = D_MODEL // 128
DK = 128
DV = 128
QK_W = HA * DK
V_W = HA * DV
QKV_W = 2 * QK_W + V_W
LRU_W = D_MODEL
NB = D_MODEL // 128
BW = LRU_W // NB
CONV_W = 4
LRU_C = 8.0
D_FF = ((8 * D_MODEL // 3 + 127) // 128) * 128
EPS = 1e-6

OFF_Z = QKV_W
OFF_BETA = OFF_Z + V_W
OFF_ALPHA = OFF_BETA + HA
OFF_LX = OFF_ALPHA + HA
OFF_LY = OFF_LX + LRU_W
OFF_GA = OFF_LY + LRU_W
OFF_GB = OFF_GA + D_MODEL
IN_COLS = OFF_GB + D_MODEL

kernel_name = "hybrid_gdn_rglru_macaron_stream_step"


def _rmsnorm(x, w):
    xf = x.astype(jnp.float32)
    y = xf * lax.rsqrt(jnp.mean(xf * xf, axis=-1, keepdims=True) + EPS)
    return (y * w.astype(jnp.float32)).astype(x.dtype)


def _l2norm(t):
    return t * lax.rsqrt(jnp.sum(t * t, axis=-1, keepdims=True) + EPS)


def _swiglu(xn, w_gu, w_down):
    gate, up = jnp.split(xn @ w_gu, 2, axis=-1)
    return (jax.nn.silu(gate) * up) @ w_down


def _causal_conv(x, buf, w):
    L = x.shape[1]
    xp = jnp.concatenate([buf.astype(x.dtype), x], axis=1)
    y = xp[:, 0:L] * w[0]
    for j in range(1, CONV_W):
        y = y + xp[:, j:j + L] * w[j]
    return y, xp[:, -(CONV_W - 1):]


def _gated_delta_rule(q, k, v, g, beta, S0, chunk):
    B, H, L, _ = q.shape
    n = L // chunk
    rs = lambda t: t.reshape((B, H, n, chunk) + t.shape[3:])
    q, k, v, g, beta = rs(q), rs(k), rs(v), rs(g), rs(beta)
    g = jnp.cumsum(g, axis=-1)
    tri_incl = jnp.tril(jnp.ones((chunk, chunk), bool))
    tri_strict = jnp.tril(jnp.ones((chunk, chunk), bool), -1)
    diff = g[..., :, None] - g[..., None, :]
    decay = jnp.where(tri_incl, jnp.exp(jnp.where(tri_incl, diff, 0.0)), 0.0)
    k_beta = k * beta[..., None]
    v_beta = v * beta[..., None]
    low = jnp.where(tri_strict, jnp.einsum('bhnid,bhnjd->bhnij', k_beta, k) * decay, 0.0)
    eye = jnp.eye(chunk, dtype=q.dtype)
    T = lax.linalg.triangular_solve(eye + low, jnp.broadcast_to(eye, low.shape),
                                    left_side=True, lower=True)
    u = jnp.einsum('bhnij,bhnjd->bhnid', T, v_beta)
    w = jnp.einsum('bhnij,bhnjd->bhnid', T, k_beta * jnp.exp(g)[..., None])
    a_intra = jnp.where(tri_incl, jnp.einsum('bhnid,bhnjd->bhnij', q, k) * decay, 0.0)
    q_dec = q * jnp.exp(g)[..., None]
    k_dec = k * jnp.exp(g[..., -1:] - g)[..., None]
    g_last = jnp.exp(g[..., -1])

    def step(S, xs):
        u_i, w_i, qd_i, kd_i, a_i, gl_i = xs
        v_new = u_i - jnp.einsum('bhck,bhkv->bhcv', w_i, S)
        o = jnp.einsum('bhck,bhkv->bhcv', qd_i, S) + jnp.einsum('bhij,bhjv->bhiv', a_i, v_new)
        S = S * gl_i[..., None, None] + jnp.einsum('bhck,bhcv->bhkv', kd_i, v_new)
        return S, o

    mv = lambda t: jnp.moveaxis(t, 2, 0)
    S, o = lax.scan(step, S0, (mv(u), mv(w), mv(q_dec), mv(k_dec), mv(a_intra), mv(g_last)))
    o = jnp.moveaxis(o, 0, 2).reshape(B, H, L, v.shape[-1])
    return o, S


def _gated_delta_branch(proj, S0, conv_buf, chunk, conv_w, A_log, dt_bias, out_norm):
    f32 = jnp.float32
    B, L, _ = proj.shape
    qkv, new_buf = _causal_conv(proj[..., :QKV_W], conv_buf, conv_w)
    qkv = jax.nn.silu(qkv.astype(f32))
    q = _l2norm(qkv[..., :QK_W].reshape(B, L, HA, DK)) * (DK ** -0.5)
    k = _l2norm(qkv[..., QK_W:2 * QK_W].reshape(B, L, HA, DK))
    v = qkv[..., 2 * QK_W:].reshape(B, L, HA, DV)
    beta = jax.nn.sigmoid(proj[..., OFF_BETA:OFF_ALPHA].astype(f32))
    g = -jnp.exp(A_log.astype(f32)) * jax.nn.softplus(
        proj[..., OFF_ALPHA:OFF_LX].astype(f32) + dt_bias.astype(f32))
    pad = (-L) % chunk

    def prep(t):
        t = jnp.moveaxis(t, 2, 1)
        return jnp.pad(t, ((0, 0), (0, 0), (pad, 0)) + ((0, 0),) * (t.ndim - 3))

    o, S = _gated_delta_rule(prep(q), prep(k), prep(v), prep(g), prep(beta), S0.astype(f32), chunk)
    o = jnp.moveaxis(o[:, :, pad:], 1, 2)
    z = proj[..., OFF_Z:OFF_BETA].astype(f32).reshape(B, L, HA, DV)
    o = o * lax.rsqrt(jnp.mean(o * o, axis=-1, keepdims=True) + EPS) * out_norm.astype(f32) * jax.nn.silu(z)
    return o.reshape(B, L, V_W).astype(proj.dtype), S, new_buf


def _linear_scan(a, b, h0):
    b = b.at[:, 0].add(a[:, 0] * h0)

    def comb(x, y):
        return x[0] * y[0], y[0] * x[1] + y[1]

    _, h = lax.associative_scan(comb, (a, b), axis=1)
    return h


def _rglru_branch(proj, h0, conv_buf, conv_w, conv_b, w_r, b_r, w_i, b_i, lam):
    f32 = jnp.float32
    B, L, _ = proj.shape
    xc, new_buf = _causal_conv(proj[..., OFF_LX:OFF_LY], conv_buf, conv_w)
    xc = xc.astype(f32) + conv_b.astype(f32)
    xb = xc.reshape(B, L, NB, BW)
    r = jax.nn.sigmoid(jnp.einsum('blnc,ncd->blnd', xb, w_r.astype(f32)).reshape(B, L, LRU_W) + b_r.astype(f32))
    i = jax.nn.sigmoid(jnp.einsum('blnc,ncd->blnd', xb, w_i.astype(f32)).reshape(B, L, LRU_W) + b_i.astype(f32))
    log_a = -LRU_C * r * jax.nn.softplus(-lam.astype(f32))
    a = jnp.exp(log_a)
    mult = jnp.sqrt(-jnp.expm1(2.0 * log_a))
    h = _linear_scan(a, mult * (i * xc), h0.astype(f32))
    y = h * jax.nn.gelu(proj[..., OFF_LY:OFF_GA].astype(f32))
    return y.astype(proj.dtype), h[:, -1], new_buf


def _trunk(x, S_all, cq_all, h_all, cx_all, chunk, weights):
    (ffn1_norm, ffn1_w_gu, ffn1_w_down, mix_norm, w_in, delta_conv_w, delta_A_log, delta_dt_bias,
     delta_out_norm, lru_conv_w, lru_conv_b, lru_w_r, lru_b_r, lru_w_i, lru_b_i, lru_lambda,
     w_branch_a, w_branch_b, w_out, ffn2_norm, ffn2_w_gu, ffn2_w_down) = weights
    new_S, new_cq, new_h, new_cx = [], [], [], []
    for l in range(DEPTH):
        x = x + 0.5 * _swiglu(_rmsnorm(x, ffn1_norm[l]), ffn1_w_gu[l], ffn1_w_down[l])
        proj = _rmsnorm(x, mix_norm[l]) @ w_in[l]
        o_a, S, cq = _gated_delta_branch(proj, S_all[l], cq_all[l], chunk, delta_conv_w[l],
                                         delta_A_log[l], delta_dt_bias[l], delta_out_norm[l])
        o_b, h, cx = _rglru_branch(proj, h_all[l], cx_all[l], lru_conv_w[l], lru_conv_b[l],
                                   lru_w_r[l], lru_b_r[l], lru_w_i[l], lru_b_i[l], lru_lambda[l])
        gate_a = jax.nn.sigmoid(proj[..., OFF_GA:OFF_GB].astype(jnp.float32))
        gate_b = jax.nn.sigmoid(proj[..., OFF_GB:IN_COLS].astype(jnp.float32))
        merged = gate_a * (o_a @ w_branch_a[l]) + gate_b * (o_b @ w_branch_b[l])
        x = x + merged.astype(x.dtype) @ w_out[l]
        x = x + 0.5 * _swiglu(_rmsnorm(x, ffn2_norm[l]), ffn2_w_gu[l], ffn2_w_down[l])
        new_S.append(S)
        new_cq.append(cq)
        new_h.append(h)
        new_cx.append(cx)
    return (x, jnp.stack(new_S).astype(S_all.dtype), jnp.stack(new_cq).astype(cq_all.dtype),
            jnp.stack(new_h).astype(h_all.dtype), jnp.stack(new_cx).astype(cx_all.dtype))


def setup_inputs(seed: int = 0) -> dict:
    key = jax.random.key(seed)
    ks = iter(jax.random.split(key, 32))
    f32 = jnp.float32
    nrm = lambda shape, scale: jax.random.normal(next(ks), shape, f32) * scale
    gain = lambda shape: 1.0 + nrm(shape, 0.02)
    lam_u = jax.random.uniform(next(ks), (DEPTH, LRU_W), f32, 0.9, 0.999)
    dt = jnp.exp(jax.random.uniform(next(ks), (DEPTH, HA), f32, np.log(1e-3), np.log(1e-1)))
    return {
        "x_prompt": nrm((BATCH, SEQ, D_MODEL), 1.0),
        "x_sample": nrm((DEC_BATCH, DEC_SEQ, D_MODEL), 1.0),
        "state_delta_S": nrm((DEPTH, DEC_BATCH, HA, DK, DV), 0.5),
        "state_delta_conv": nrm((DEPTH, DEC_BATCH, CONV_W - 1, QKV_W), 1.0),
        "state_lru_h": nrm((DEPTH, DEC_BATCH, LRU_W), 0.5),
        "state_lru_conv": nrm((DEPTH, DEC_BATCH, CONV_W - 1, LRU_W), 1.0),
        "meta_tokens": nrm((N_META, D_MODEL), 1.0),
        "ffn1_norm": gain((DEPTH, D_MODEL)),
        "ffn1_w_gu": nrm((DEPTH, D_MODEL, 2 * D_FF), D_MODEL ** -0.5),
        "ffn1_w_down": nrm((DEPTH, D_FF, D_MODEL), D_FF ** -0.5),
        "mix_norm": gain((DEPTH, D_MODEL)),
        "w_in": nrm((DEPTH, D_MODEL, IN_COLS), D_MODEL ** -0.5),
        "delta_conv_w": nrm((DEPTH, CONV_W, QKV_W), CONV_W ** -0.5),
        "delta_A_log": jnp.log(jax.random.uniform(next(ks), (DEPTH, HA), f32, 1.0, 16.0)),
        "delta_dt_bias": dt + jnp.log(-jnp.expm1(-dt)),
        "delta_out_norm": gain((DEPTH, DV)),
        "lru_conv_w": nrm((DEPTH, CONV_W, LRU_W), CONV_W ** -0.5),
        "lru_conv_b": nrm((DEPTH, LRU_W), 0.01),
        "lru_w_r": nrm((DEPTH, NB, BW, BW), BW ** -0.5),
        "lru_b_r": nrm((DEPTH, LRU_W), 0.01),
        "lru_w_i": nrm((DEPTH, NB, BW, BW), BW ** -0.5),
        "lru_b_i": nrm((DEPTH, LRU_W), 0.01),
        "lru_lambda": jnp.log(lam_u) - jnp.log1p(-lam_u),
        "w_branch_a": nrm((DEPTH, V_W, D_MODEL), V_W ** -0.5),
        "w_branch_b": nrm((DEPTH, LRU_W, D_MODEL), LRU_W ** -0.5),
        "w_out": nrm((DEPTH, D_MODEL, D_MODEL), D_MODEL ** -0.5),
        "ffn2_norm": gain((DEPTH, D_MODEL)),
        "ffn2_w_gu": nrm((DEPTH, D_MODEL, 2 * D_FF), D_MODEL ** -0.5),
        "ffn2_w_down": nrm((DEPTH, D_FF, D_MODEL), D_FF ** -0.5),
        "final_norm": gain((D_MODEL,)),
    }


def reference(x_prompt, x_sample, state_delta_S, state_delta_conv, state_lru_h, state_lru_conv,
              meta_tokens, ffn1_norm, ffn1_w_gu, ffn1_w_down, mix_norm, w_in, delta_conv_w,
              delta_A_log, delta_dt_bias, delta_out_norm, lru_conv_w, lru_conv_b, lru_w_r, lru_b_r,
              lru_w_i, lru_b_i, lru_lambda, w_branch_a, w_branch_b, w_out, ffn2_norm, ffn2_w_gu,
              ffn2_w_down, final_norm):
    weights = (ffn1_norm, ffn1_w_gu, ffn1_w_down, mix_norm, w_in, delta_conv_w, delta_A_log,
               delta_dt_bias, delta_out_norm, lru_conv_w, lru_conv_b, lru_w_r, lru_b_r, lru_w_i,
               lru_b_i, lru_lambda, w_branch_a, w_branch_b, w_out, ffn2_norm, ffn2_w_gu, ffn2_w_down)
    dt = x_prompt.dtype
    B = x_prompt.shape[0]
    meta = jnp.broadcast_to(meta_tokens.astype(dt)[None], (B, N_META, D_MODEL))
    xp = jnp.concatenate([meta, x_prompt], axis=1)
    xp, p_S, p_cq, p_h, p_cx = _trunk(
        xp,
        jnp.zeros((DEPTH, B, HA, DK, DV), dt),
        jnp.zeros((DEPTH, B, CONV_W - 1, QKV_W), dt),
        jnp.zeros((DEPTH, B, LRU_W), dt),
        jnp.zeros((DEPTH, B, CONV_W - 1, LRU_W), dt),
        CHUNK, weights)
    y_prompt = _rmsnorm(xp, final_norm)[:, N_META:]
    xs, s_S, s_cq, s_h, s_cx = _trunk(x_sample, state_delta_S, state_delta_conv, state_lru_h,
                                      state_lru_conv, x_sample.shape[1], weights)
    y_sample = _rmsnorm(xs, final_norm)
    return (y_prompt, y_sample, p_S, p_cq, p_h, p_cx, s_S, s_cq, s_h, s_cx)
```

```python
import contextlib
import math
import numpy as np
import concourse.bass as bass
import concourse.mybir as mybir
from concourse.bass_utils import run_bass_kernel_spmd

F32 = mybir.dt.float32
BF16 = mybir.dt.bfloat16
AF = mybir.ActivationFunctionType
ALU = mybir.AluOpType

D = 1024
NL_FULL = 4
SEQ = 2048
NMETA = 16
DFF = 2816
NFC = 22
QKV = 3072
OFF_Z = 3072
OFF_BA = 4096
OFF_LX = 4112
OFF_LY = OFF_LX + 1024
OFF_GA = OFF_LY + 1024
OFF_GB = OFF_GA + 1024
INC = OFF_GB + 1024
EPS = 1e-6
TMAX = 448
SAME_ENGINE_SYNC = True
SEM_ROT = 30000


class Buf:
    def __init__(self, t, key):
        self.t = t
        self.key = key

    def __getitem__(self, idx):
        return self.t[idx]


class Sched:
    def __init__(self, nc, es):
        self.nc = nc
        self.es = es
        self.nsem = 0
        self.eng = {}
        for name, h in (("pe", nc.tensor), ("act", nc.scalar), ("dve", nc.vector),
                        ("pool", nc.gpsimd), ("sp", nc.sync)):
            self.eng[name] = dict(h=h, sem=None, cnt=0, seen={})
            self._new_sem(name)
        self.last_w = {}
        self.readers = {}
        self.dma_sems = {}
        self.ninst = 0
        self.excl = set()
        self.phase = 'init'
        self.pe_log = []

    def _mk(self, owner):
        nm = f"s{self.nsem}_{owner}"
        self.nsem += 1
        h = self.es.enter_context(self.nc.semaphore(nm))
        return (nm, h, owner)

    def _new_sem(self, name):
        e = self.eng[name]
        e["sem"] = self._mk(name)
        e["cnt"] = 0

    def _deps(self, r, w, ename=None):
        d = {}

        def add(tok):
            if tok is None:
                return
            k = tok[0][0]
            if k not in d or d[k][1] < tok[1]:
                d[k] = tok
        for k in r:
            add(self.last_w.get(k))
            if k in self.excl:
                for tok in self.readers.get(k, {}).values():
                    if tok[0][2] != ename:
                        add(tok)
        for k in w:
            add(self.last_w.get(k))
            for tok in self.readers.get(k, {}).values():
                add(tok)
        return d

    def _wait(self, ename, deps):
        e = self.eng[ename]
        for k, (s, v) in deps.items():
            if s[2] == ename and (ename == "pe" or not SAME_ENGINE_SYNC):
                continue
            if e["seen"].get(k, 0) >= v:
                continue
            e["h"].wait_ge(s[1], v)
            e["seen"][k] = v

    def _record(self, tok, r, w):
        for k in r:
            self.readers.setdefault(k, {})[tok[0][0]] = tok
        for k in w:
            self.last_w[k] = tok
            self.readers[k] = {}

    def op(self, ename, fns, r=(), w=()):
        self._wait(ename, self._deps(r, w, ename))
        e = self.eng[ename]
        if not isinstance(fns, (list, tuple)):
            fns = [fns]
        inst = None
        if ename == 'pe':
            self.pe_log.extend([self.phase] * len(fns))
        for f in fns:
            inst = f(e["h"])
            self.ninst += 1
        e["cnt"] += 1
        inst.then_inc(e["sem"][1], 1)
        tok = (e["sem"], e["cnt"])
        self._record(tok, r, w)
        if e["cnt"] >= SEM_ROT:
            self._new_sem(ename)

    def dma(self, qname, out, in_, r=(), w=(), skey=None):
        self._wait(qname, self._deps(r, w))
        if skey not in self.dma_sems:
            self.dma_sems[skey] = [self._mk("dma"), 0]
        ds = self.dma_sems[skey]
        if not isinstance(out, (list, tuple)):
            out, in_ = [out], [in_]
        for o_, i_ in zip(out, in_):
            ds[1] += 16
            self.eng[qname]["h"].dma_start(out=o_, in_=i_).then_inc(ds[0][1], 16)
            self.ninst += 1
        self._record((ds[0], ds[1]), r, w)

    def finish(self):
        sp = self.eng["sp"]["h"]
        for ds in self.dma_sems.values():
            sp.wait_ge(ds[0][1], ds[1])
        for name in ("pe", "act", "dve"):
            e = self.eng[name]
            if e["cnt"] > 0:
                sp.wait_ge(e["sem"][1], e["cnt"])


def make_blocks():
    blocks = []
    bounds = [0, 400, 848, 1296, 1744, 2064]
    for b in range(5):
        p0, p1 = bounds[b], bounds[b + 1]
        n = p1 - p0
        segs = []
        pos = p0
        if b == 0:
            segs.append((0, 0, 16))
            pos = 16
        while pos < p1:
            segs.append((0, pos - p0, 64))
            pos += 64
        srcs = []
        if b == 0:
            srcs.append(("meta", 0, 16))
            srcs.append(("xp", 0, n - 16))
        else:
            srcs.append(("xp", p0 - 16, n))
        runs = [(0, 0, n)]
        T = n
        if b == 4:
            runs.append((1, T, 32))
            segs.append((1, T, 32))
            runs.append((2, T + 32, 32))
            segs.append((2, T + 32, 32))
            srcs.append(("xs", 0, 64))
            T += 64
        pruns = []
        for i, (sq, c0, nn) in enumerate(runs):
            pruns.append((sq, c0, nn, c0 + 3 * i))
        blocks.append(dict(T=T, runs=pruns, segs=segs, srcs=srcs, p0=p0, last=(b == 4)))
    return blocks


class _Stop(Exception):
    pass


def build(nl=NL_FULL, blocks_sel=None, stop=None):
    nc = bass.Bass("TRN2", target_bir_lowering=False)
    es = contextlib.ExitStack()

    def din(name, shape):
        return nc.dram_tensor(name, list(shape), F32, kind="ExternalInput").ap()

    def dout(name, shape):
        return nc.dram_tensor(name, list(shape), F32, kind="ExternalOutput").ap()

    xp = din("xp", [SEQ, D])
    xs = din("xs", [64, D])
    st_S = din("st_S", [NL_FULL, 2, 8, 128, 128])
    st_cq = din("st_cq", [NL_FULL, 2, 3, QKV])
    st_h = din("st_h", [NL_FULL, 2, D])
    st_cx = din("st_cx", [NL_FULL, 2, 3, D])
    meta = din("meta", [NMETA, D])
    ffn1_norm = din("ffn1_norm", [NL_FULL, D])
    ffn1_w_gu = din("ffn1_w_gu", [NL_FULL, D, 2 * DFF])
    ffn1_w_down = din("ffn1_w_down", [NL_FULL, DFF, D])
    mix_norm = din("mix_norm", [NL_FULL, D])
    w_in = din("w_in", [NL_FULL, D, INC])
    delta_conv_w = din("delta_conv_w", [NL_FULL, 4, QKV])
    delta_A_log = din("delta_A_log", [NL_FULL, 8])
    delta_dt_bias = din("delta_dt_bias", [NL_FULL, 8])
    delta_out_norm = din("delta_out_norm", [NL_FULL, 128])
    lru_conv_w = din("lru_conv_w", [NL_FULL, 4, D])
    lru_conv_b = din("lru_conv_b", [NL_FULL, D])
    lru_w_r = din("lru_w_r", [NL_FULL, 8, 128, 128])
    lru_b_r = din("lru_b_r", [NL_FULL, D])
    lru_w_i = din("lru_w_i", [NL_FULL, 8, 128, 128])
    lru_b_i = din("lru_b_i", [NL_FULL, D])
    lru_lambda = din("lru_lambda", [NL_FULL, D])
    w_branch_a = din("w_branch_a", [NL_FULL, D, D])
    w_branch_b = din("w_branch_b", [NL_FULL, D, D])
    w_out = din("w_out", [NL_FULL, D, D])
    ffn2_norm = din("ffn2_norm", [NL_FULL, D])
    ffn2_w_gu = din("ffn2_w_gu", [NL_FULL, D, 2 * DFF])
    ffn2_w_down = din("ffn2_w_down", [NL_FULL, DFF, D])
    final_norm = din("final_norm", [1, D])

    y_p = dout("y_p", [SEQ, D])
    y_s = dout("y_s", [64, D])
    o_pS = dout("o_pS", [NL_FULL, 8, 128, 128])
    o_pcq = dout("o_pcq", [NL_FULL, 3, QKV])
    o_ph = dout("o_ph", [NL_FULL, D])
    o_pcx = dout("o_pcx", [NL_FULL, 3, D])
    o_sS = dout("o_sS", [NL_FULL, 2, 8, 128, 128])
    o_scq = dout("o_scq", [NL_FULL, 2, 3, QKV])
    o_sh = dout("o_sh", [NL_FULL, 2, D])
    o_scx = dout("o_scx", [NL_FULL, 2, 3, D])

    TPB = 63 * nl
    wscr = nc.dram_tensor("wscr", [TPB, 128, 4096], BF16, kind="Internal").ap()
    S = Sched(nc, es)

    def stage(name):
        if stop == name:
            raise _Stop()

    def sb(name, shape, dt=F32):
        return Buf(es.enter_context(nc.sbuf_tensor(name, list(shape), dt)), name)

    def pm(name, shape, dt=F32):
        S.excl.add(name)
        return Buf(es.enter_context(nc.psum_tensor(name, list(shape), dt)), name)

    def mm(out, lhsT, rhs, start=True, stop=True):
        return lambda e: e.matmul(out, lhsT=lhsT, rhs=rhs, start=start, stop=stop)

    def tr(out, in_, ident):
        return lambda e: e.transpose(out, in_, ident)

    def act(out, in_, func, bias=None, scale=None):
        kw = {}
        if bias is not None:
            kw["bias"] = bias
        if scale is not None:
            kw["scale"] = scale
        return lambda e: e.activation(out=out, in_=in_, func=func, **kw)

    def tt(out, in0, in1, op):
        return lambda e: e.tensor_tensor(out=out, in0=in0, in1=in1, op=op)

    def ts(out, in0, s1, op0, s2=None, op1=None):
        if op1 is None:
            return lambda e: e.tensor_scalar(out=out, in0=in0, scalar1=s1, scalar2=None, op0=op0)
        return lambda e: e.tensor_scalar(out=out, in0=in0, scalar1=s1, scalar2=s2, op0=op0, op1=op1)

    def stt(out, in0, scalar, in1, op0, op1):
        return lambda e: e.scalar_tensor_tensor(out=out, in0=in0, scalar=scalar, in1=in1, op0=op0, op1=op1)

    def cp(out, in_):
        return lambda e: e.tensor_copy(out=out, in_=in_)

    def mset(ap, v):
        return lambda e: e.memset(ap, v)

    ident_f = sb("ident_f", [128, 128])
    ident_b = sb("ident_b", [128, 128], BF16)
    ones_f = sb("ones_f", [128, 128])
    ones_b = sb("ones_b", [128, 128], BF16)
    eps_t = sb("eps_t", [128, 1])
    one_t = sb("one_t", [128, 1])
    lnq_t = sb("lnq_t", [128, 1])
    triu = sb("triu", [128, 8, 64])
    maskL = sb("maskL", [128, 8, 64])
    identX = sb("identX", [128, 8, 64])
    VP = sb("VP", [128, 32, 32])
    c8 = sb("c8", [128, 32])
    dtb = sb("dtb", [128, NL_FULL * 8])
    negA = sb("negA", [128, NL_FULL * 8])

    x = sb("x", [128, 8, TMAX])
    xn = sb("xn", [128, 8, TMAX], BF16)
    AR = sb("AR", [128, 24, TMAX], BF16)
    NT = 12
    tmps = [sb(f"tmp{i}", [128, TMAX]) for i in range(NT)]
    tmpb = [sb(f"tmpb{i}", [128, TMAX], BF16) for i in range(2)]
    stgs = [sb(f"stg{i}", [128, TMAX + 12], BF16) for i in range(2)]
    dgs = [sb(f"dg{i}", [128, 4, 128], BF16) for i in range(2)]
    dg_i = [0]
    hhs = [sb(f"hh{i}", [128, TMAX]) for i in range(2)]
    xtm_in = sb("xtm_in", [128, D])
    xtm_out = xtm_in
    ofm = sb("ofm", [128, 8, TMAX])
    wba = sb("wba", [128, 8, 16], BF16)
    wri = sb("wri", [128, 16, 128], BF16)
    RING = 4
    WSLOT = 4096
    ring = [sb(f"wr{i}", [128, WSLOT], BF16) for i in range(RING)]
    ST_P = [sb(f"STP{l}", [128, 8, 128]) for l in range(nl)]
    ST_S = [sb(f"STS{s}", [128, 8, 128]) for s in range(2)]
    SBb = [sb(f"SB{s}", [128, 8, 128], BF16) for s in range(3)]
    histP = [sb(f"histP{l}", [128, 32, 3]) for l in range(nl)]
    histS = [sb(f"histS{s}", [128, 32, 3]) for s in range(2)]
    hstP = [sb(f"hstP{l}", [128, 8]) for l in range(nl)]
    hstS = [sb(f"hstS{s}", [128, 8]) for s in range(2)]
    TT = sb("TT", [128, 104])
    TTt = sb("TTt", [128, 128])
    Dm = sb("Dm", [128, 8, 64])
    dL = sb("dL", [128, 8, 64])
    dU = sb("dU", [128, 8, 64])
    Gtri = dU
    Am = [sb(f"Am{i}", [128, 8, 64], BF16) for i in range(2)]
    UX = [sb(f"UX{i}", [128, 8, 128], BF16) for i in range(2)]
    EGs = [sb(f"EG{i}", [128, 8, 64]) for i in range(2)]
    Xb = sb("Xb", [128, 8, 64], BF16)
    aT = sb("aT", [128, 8, 64], BF16)
    vb = sb("vb", [128, 8, 128], BF16)
    kbg = sb("kbg", [128, 8, 128], BF16)
    kd = sb("kd", [128, 8, 128], BF16)
    wTns = [sb(f"wTn{i}", [128, 8, 64], BF16) for i in range(2)]
    vnb = sb("vnb", [128, 8, 128], BF16)
    qds = [sb(f"qd{i}", [128, 8, 64], BF16) for i in range(2)]
    scl = {n: sb(f"sc_{n}", [128, 8]) for n in ("beta", "al", "g", "gc", "nb", "egc", "bge", "edl", "tc")}

    PA = pm("PA", [128, 512])
    PB = pm("PB", [128, 512])
    PC = pm("PC", [128, 512])
    PD = pm("PD", [128, 512])
    PE2 = pm("PE2", [128, 1024])
    PF2 = pm("PF2", [128, 1024])
    banks = [PA, PB, PC, PD]
    bank_i = [0]

    def bank():
        b = banks[bank_i[0] % 4]
        bank_i[0] += 1
        return b

    tmp_i = [0]

    def tmp():
        t = tmps[tmp_i[0] % NT]
        tmp_i[0] += 1
        return t

    def bfv(t):
        return t.t[:, :].bitcast(BF16)

    tb_i = [0]

    def tmpbf():
        t = tmpb[tb_i[0] % 2]
        tb_i[0] += 1
        return t

    stg_i = [0]
    hh_i = [0]

    def pv(P, C, w=None):
        w = C if w is None else w
        return P.t[:, 0:8 * w].rearrange("p (h c) -> p h c", c=w)

    wl_n = [0]
    wl_blk = [0]
    wl_pass = [0]
    nblk_done = [0]

    def wload(dram_views, shape):
        i = wl_n[0] % RING
        wl_n[0] += 1
        slot = ring[i]
        n = 1
        for s_ in shape[1:]:
            n *= s_
        flat = slot.t[:, 0:n]
        keys = [(slot.key, 0), (slot.key, 1)]
        if len(shape) == 3:
            view = flat.rearrange("p (a b) -> p a b", b=shape[2])
        elif len(shape) == 4:
            view = flat.rearrange("p (a b c) -> p a b c", b=shape[2], c=shape[3])
        else:
            view = flat
        tid = wl_blk[0]
        wl_blk[0] += 1
        if wl_pass[0] == 0:
            if isinstance(dram_views, (list, tuple)):
                S.dma("pool", [view[:, :, g_, :] for g_ in range(len(dram_views))], list(dram_views), r=(), w=keys,
                      skey=slot.key)
            else:
                S.dma("pool", view, dram_views, r=(), w=keys, skey=slot.key)
            S.dma("sp", wscr[tid, :, 0:n], flat, r=keys, w=[("scr", tid)], skey=("scrst", i))
        else:
            S.dma("pool", flat, wscr[tid, :, 0:n], r=[("scr", tid)], w=keys, skey=slot.key)
        return view, keys

    S.op("dve", mset(ones_f[:, :], 1.0), w=[ones_f.key])
    S.op("dve", cp(ones_b[:, :], ones_f[:, :]), r=[ones_f.key], w=[ones_b.key])
    S.op("dve", mset(eps_t[:, :], EPS), w=[eps_t.key])
    S.op("dve", mset(one_t[:, :], 1.0), w=[one_t.key])
    S.op("dve", mset(lnq_t[:, :], -0.5 * math.log(128.0)), w=[lnq_t.key])
    S.op("pool", lambda e: e.affine_select(out=ident_f[:, :], in_=ones_f[:, :], pattern=[[-1, 128]],
                                            compare_op=ALU.is_equal, fill=0.0, base=0, channel_multiplier=1),
         r=[ones_f.key], w=[ident_f.key])
    S.op("dve", cp(ident_b[:, :], ident_f[:, :]), r=[ident_f.key], w=[ident_b.key])
    for h in range(8):
        S.op("pool", lambda e, h=h: e.affine_select(out=triu[0:64, h, :], in_=ones_f[0:64, 0:64], pattern=[[1, 64]],
                                                    compare_op=ALU.is_ge, fill=0.0, base=0, channel_multiplier=-1),
             r=[ones_f.key], w=[triu.key])
        S.op("pool", lambda e, h=h: e.affine_select(out=maskL[0:64, h, :], in_=ones_f[0:64, 0:64], pattern=[[-1, 64]],
                                                    compare_op=ALU.is_gt, fill=0.0, base=0, channel_multiplier=1),
             r=[ones_f.key], w=[maskL.key])
        S.op("pool", lambda e, h=h: e.affine_select(out=identX[0:64, h, :], in_=ones_f[0:64, 0:64], pattern=[[-1, 64]],
                                                    compare_op=ALU.is_equal, fill=0.0, base=0, channel_multiplier=1),
             r=[ones_f.key], w=[identX.key])
    for mk in (triu, maskL, identX):
        S.dma("sp", mk[64:128, :, :], mk[0:64, :, :], r=[mk.key], w=[(mk.key, 'hi')], skey=(mk.key, 'hi'))
    maskU = triu

    vecs = []
    for l in range(NL_FULL):
        vecs += [ffn1_norm[l:l + 1, :], mix_norm[l:l + 1, :], ffn2_norm[l:l + 1, :], lru_conv_b[l:l + 1, :],
                 lru_b_r[l:l + 1, :], lru_b_i[l:l + 1, :], lru_lambda[l:l + 1, :]]
    vecs.append(final_norm[0:1, :])

    def vcol(vi, dc):
        return VP[:, (vi % 4) * 8 + dc, 16 + vi // 4: 17 + vi // 4]

    def convw(l, cidx, j):
        return VP[:, cidx, l * 4 + j: l * 4 + j + 1]

    S.op("dve", mset(xtm_in[:, :], 0.0), w=[xtm_in.key])
    PVP = PE2.t[:, 0:1024].rearrange("p (c r) -> p c r", r=32)
    for g in range(4):
        for l in range(NL_FULL):
            if g < 3:
                src = delta_conv_w[l, :, g * 1024:(g + 1) * 1024]
            else:
                src = lru_conv_w[l, :, :]
            S.dma("sp", xtm_in[l * 4:(l + 1) * 4, :], src, w=[xtm_in.key], skey="xtm_in")
        for vi in range(g, len(vecs), 4):
            S.dma("sp", xtm_in[16 + vi // 4:17 + vi // 4, :], vecs[vi], w=[xtm_in.key], skey="xtm_in")
        if g == 0:
            S.dma("sp", xtm_in[24:25, 0:512], delta_out_norm.rearrange("l d -> (l d)").unsqueeze(0),
                  w=[xtm_in.key], skey="xtm_in")
        S.op("pe", [tr(PVP[:, 8 * g + c, 0:25], xtm_in[0:25, c * 128:(c + 1) * 128], ident_f[0:25, 0:25])
                    for c in range(8)], r=[xtm_in.key, ident_f.key], w=[PE2.key])
    S.op("dve", cp(VP[:, :, 0:25], PVP[:, :, 0:25]), r=[PE2.key], w=[VP.key])

    def onw(l):
        return VP[:, l, 24:25]

    for l in range(NL_FULL):
        vi = l * 7 + 6
        for dc in range(8):
            S.op("act", act(c8[:, l * 8 + dc:l * 8 + dc + 1], vcol(vi, dc), AF.Exp, scale=-1.0),
                 r=[VP.key], w=[c8.key])
    S.op("act", act(c8[:, :], c8[:, :], AF.Ln, bias=one_t[:, 0:1]), r=[c8.key, one_t.key], w=[c8.key])
    S.op("dve", ts(c8[:, :], c8[:, :], -8.0, ALU.mult), r=[c8.key], w=[c8.key])
    S.dma("sp", dtb[:, :], delta_dt_bias.rearrange("l h -> (l h)").partition_broadcast(128), w=[dtb.key], skey="dtb")
    S.dma("sp", negA[:, :], delta_A_log.rearrange("l h -> (l h)").partition_broadcast(128), w=[negA.key], skey="negA")
    S.op("act", act(negA[:, :], negA[:, :], AF.Exp), r=[negA.key], w=[negA.key])
    S.op("dve", ts(negA[:, :], negA[:, :], -1.0, ALU.mult), r=[negA.key], w=[negA.key])
    for l in range(nl):
        S.op("dve", mset(ST_P[l][:, :, :], 0.0), w=[ST_P[l].key])
        S.op("dve", mset(histP[l][:, :, :], 0.0), w=[histP[l].key])
        S.op("dve", mset(hstP[l][:, :], 0.0), w=[hstP[l].key])

    xkeys = [("x", dc) for dc in range(8)]
    xnkeys = [("xn", dc) for dc in range(8)]

    def ark(i):
        return ("AR", i)

    def rms_stats(T, src_fn, src_keys, nchunk, scale):
        ps = bank()
        for dc in range(nchunk):
            t = tmp()
            S.op("act", act(bfv(t)[:, :T], src_fn(dc), AF.Square), r=[src_keys[dc]], w=[t.key])
            S.op("pe", mm(ps[:, :T], ones_b[:, :], bfv(t)[:, :T], dc == 0, dc == nchunk - 1),
                 r=[t.key, ones_b.key], w=[ps.key])
        ln = tmp()
        S.op("act", act(ln[:, :T], ps[:, :T], AF.Ln, bias=eps_t[:, 0:1], scale=scale),
             r=[ps.key, eps_t.key], w=[ln.key])
        rs = tmp()
        S.op("act", act(rs[:, :T], ln[:, :T], AF.Exp, scale=-0.5), r=[ln.key], w=[rs.key])
        return rs

    def norm_to_xn(T, vi):
        S.phase = 'norm'
        rs = rms_stats(T, lambda dc: x[:, dc, :T], xkeys, 8, 1.0 / D)
        for dc in range(8):
            S.op("dve", stt(xn[:, dc, :T], x[:, dc, :T], vcol(vi, dc), rs[:, :T], ALU.mult, ALU.mult),
                 r=[xkeys[dc], rs.key, VP.key], w=[xnkeys[dc]])

    def ffn(l, T, vi, w_gu, w_down):
        norm_to_xn(T, vi)
        S.phase = 'ffn_gu'
        gu = w_gu[l].rearrange("(dc p) (g f) -> p dc g f", p=128, g=2)
        for j in range(11):
            slot, sk = wload([gu[:, :, 0, j * 256:(j + 1) * 256], gu[:, :, 1, j * 256:(j + 1) * 256]], [128, 8, 2, 256])
            for sub in range(2):
                f = 2 * j + sub
                psg = bank()
                S.op("pe", [mm(psg[:, :T], slot[:, dc, 0, sub * 128:(sub + 1) * 128], xn[:, dc, :T], dc == 0, dc == 7)
                            for dc in range(8)], r=sk + xnkeys, w=[psg.key])
                psu = bank()
                S.op("pe", [mm(psu[:, :T], slot[:, dc, 1, sub * 128:(sub + 1) * 128], xn[:, dc, :T], dc == 0, dc == 7)
                            for dc in range(8)], r=sk + xnkeys, w=[psu.key])
                sg = tmp()
                S.op("act", act(sg[:, :T], psg[:, :T], AF.Silu), r=[psg.key], w=[sg.key])
                S.op("dve", tt(AR[:, f, :T], sg[:, :T], psu[:, :T], ALU.mult), r=[sg.key, psu.key], w=[ark(f)])
        S.phase = 'ffn_dn'
        dn = w_down[l].rearrange("(fc p) m -> p fc m", p=128)
        for j in range(8):
            slot, sk = wload(dn[:, :, j * 128:(j + 1) * 128], [128, NFC, 128])
            psd = bank()
            S.op("pe", [mm(psd[:, :T], slot[:, fc, :], AR[:, fc, :T], fc == 0, fc == NFC - 1) for fc in range(NFC)],
                 r=sk + [ark(f) for f in range(NFC)], w=[psd.key])
            S.op("dve", stt(x[:, j, :T], psd[:, :T], 0.5, x[:, j, :T], ALU.mult, ALU.add),
                 r=[psd.key, xkeys[j]], w=[xkeys[j]])

    def hist_of(l, seq):
        return histP[l] if seq == 0 else histS[seq - 1]

    def hst_of(l, seq):
        return hstP[l] if seq == 0 else hstS[seq - 1]

    def ST_of(l, seq):
        return ST_P[l] if seq == 0 else ST_S[seq - 1]

    def proj(slot, sk, sub, T, ps=None):
        if ps is None:
            ps = bank()
        S.op("pe", [mm(ps[:, :T], slot[:, dc, sub * 128:(sub + 1) * 128], xn[:, dc, :T], dc == 0, dc == 7)
                    for dc in range(8)], r=sk + xnkeys, w=[ps.key])
        return ps

    def conv_prep(l, cidx, ps, blk):
        st = stgs[stg_i[0] % 2]
        stg_i[0] += 1
        dg = dgs[dg_i[0] % 2]
        dg_i[0] += 1
        S.op("dve", tt(dg[:, :, :], ident_b[:, :].unsqueeze(1).to_broadcast([128, 4, 128]),
                       VP[:, cidx, l * 4:l * 4 + 4].unsqueeze(2).to_broadcast([128, 4, 128]), ALU.mult),
             r=[ident_b.key, VP.key], w=[dg.key])
        for (seq, c0, n, p0) in blk["runs"]:
            hist = hist_of(l, seq)
            hk = (hist.key, cidx)
            S.op("dve", cp(st[:, p0:p0 + 3], hist[:, cidx, :]), r=[hk], w=[st.key])
            S.op("act", act(st[:, p0 + 3:p0 + 3 + n], ps[:, c0:c0 + n], AF.Identity), r=[ps.key], w=[st.key])
            S.op("dve", cp(hist[:, cidx, :], ps[:, c0 + n - 3:c0 + n]), r=[ps.key, st.key], w=[hk])
        return st, dg

    def conv_mm(st, dg, blk, pc):
        for (seq, c0, n, p0) in blk["runs"]:
            S.op("pe", [mm(pc[:, c0:c0 + n], dg[:, j, :], st[:, p0 + j:p0 + j + n], j == 0, j == 3) for j in range(4)],
                 r=[dg.key, st.key], w=[pc.key])
        return pc

    def conv_chunk(l, cidx, ps, blk, pc=None):
        st, dg = conv_prep(l, cidx, ps, blk)
        if pc is None:
            pc = bank()
        return conv_mm(st, dg, blk, pc)

    def load_sample_state(l):
        PAx = PA.t[:, 0:128].rearrange("p (c r) -> p c r", r=4)
        for s in range(2):
            for g in range(4):
                if g < 3:
                    S.dma("sp", xtm_in[0:3, :], st_cq[l, s, :, g * 1024:(g + 1) * 1024], w=[xtm_in.key], skey="xtm_in")
                else:
                    S.dma("sp", xtm_in[0:3, :], st_cx[l, s, :, :], w=[xtm_in.key], skey="xtm_in")
                if g == 0:
                    S.dma("sp", xtm_in[3:4, :], st_h[l, s:s + 1, :], w=[xtm_in.key], skey="xtm_in")
                S.op("pe", [tr(PAx[:, 8 * g + c, 0:4], xtm_in[0:4, c * 128:(c + 1) * 128], ident_f[0:4, 0:4])
                            for c in range(8)], r=[xtm_in.key, ident_f.key], w=[PA.key])
            hk = [(histS[s].key, c) for c in range(32)]
            S.op("dve", cp(histS[s][:, :, :], PAx[:, :, 0:3]), r=[PA.key], w=hk)
            S.op("dve", cp(hstS[s][:, :], PAx[:, 0:8, 3]), r=[PA.key], w=[hstS[s].key])
            S.dma("sp", ST_S[s][:, :, :], st_S[l, s].rearrange("h p v -> p h v"), w=[ST_S[s].key], skey=ST_S[s].key)

    def write_tails(l, seq):
        hist = hist_of(l, seq)
        hst = hst_of(l, seq)
        ST = ST_of(l, seq)
        hk = [(hist.key, c) for c in range(32)]
        S.op("dve", cp(TT[:, 0:72].rearrange("p (r c) -> p r c", r=3), hist[:, 0:24, :].rearrange("p c r -> p r c")),
             r=hk, w=[TT.key])
        S.op("dve", cp(TT[:, 72:96].rearrange("p (r c) -> p r c", r=3), hist[:, 24:32, :].rearrange("p c r -> p r c")),
             r=hk, w=[TT.key])
        S.op("dve", cp(TT[:, 96:104], hst[:, :]), r=[hst.key], w=[TT.key])
        S.op("pe", tr(PA[0:104, 0:128], TT[:, 0:104], ident_f[:, :]), r=[TT.key, ident_f.key], w=[PA.key])
        S.op("act", act(TTt[0:104, :], PA[0:104, 0:128], AF.Identity), r=[PA.key], w=[TTt.key])
        if seq == 0:
            cq, cx, hh_, SS = o_pcq[l], o_pcx[l], o_ph[l:l + 1, :], o_pS[l]
        else:
            s = seq - 1
            cq, cx, hh_, SS = o_scq[l, s], o_scx[l, s], o_sh[l, s:s + 1, :], o_sS[l, s]
        for r_ in range(3):
            S.dma("sp", cq[r_, :].rearrange("(c f) -> c f", f=128), TTt[r_ * 24:(r_ + 1) * 24, :],
                  r=[TTt.key], skey="TTt")
            S.dma("sp", cx[r_, :].rearrange("(c f) -> c f", f=128), TTt[72 + r_ * 8:72 + (r_ + 1) * 8, :],
                  r=[TTt.key], skey="TTt")
        S.dma("sp", hh_.rearrange("o (c f) -> (o c) f", f=128), TTt[96:104, :], r=[TTt.key], skey="TTt")
        S.dma("sp", SS.rearrange("h p v -> p h v"), ST[:, :, :], r=[ST.key], skey=ST.key)

    def delta_pair(l, chs):
        n = len(chs)
        C = chs[0][2]
        R = C if n == 1 else 128
        nlev = int(math.log2(C))
        pos = [64 * i for i in range(n)]
        bta, al, g, gc, nb, egc, bge, edl, tcs = (scl[k_] for k_ in ("beta", "al", "g", "gc", "nb", "egc", "bge", "edl", "tc"))
        lo, hi = l * 8, l * 8 + 8
        mhi = [(triu.key, 'hi'), (maskL.key, 'hi'), (identX.key, 'hi')]

        def bc(ap, w_):
            return ap.unsqueeze(2).to_broadcast([R, 8, w_])

        def mmt(out, lhsT, rhs, tp, start=True, stop=True):
            return lambda e: e.matmul(out, lhsT=lhsT, rhs=rhs, start=start, stop=stop, tile_position=tp)

        def trt(out, in_, ident, tp):
            return lambda e: e.transpose(out, in_, ident, tile_position=tp)

        GB = [PB, PD]
        kks = [[ark(8 + h) for h in range(8)]]
        kk = [ark(8 + h) for h in range(8)]
        qk = [ark(h) for h in range(8)]
        vk = [ark(16 + h) for h in range(8)]
        fl = []
        for (seq, off, _), po in zip(chs, pos):
            fl += [mmt(PA[po:po + C, 0:16], xn[:, dc, off:off + C], wba[:, dc, :], (0, po), dc == 0, dc == 7) for dc in range(8)]
        S.op("pe", fl, r=xnkeys + [wba.key], w=[PA.key])
        S.op("act", act(bta[0:R, :], PA[0:R, 0:8], AF.Exp, scale=-1.0), r=[PA.key], w=[bta.key])
        S.op("act", act(bta[0:R, :], bta[0:R, :], AF.Ln, bias=one_t[0:R, 0:1]), r=[bta.key, one_t.key], w=[bta.key])
        S.op("act", act(bta[0:R, :], bta[0:R, :], AF.Exp, scale=-1.0), r=[bta.key], w=[bta.key])
        S.op("dve", tt(al[0:R, :], PA[0:R, 8:16], dtb[0:R, lo:hi], ALU.add), r=[PA.key, dtb.key], w=[al.key])
        S.op("act", act(al[0:R, :], al[0:R, :], AF.Exp), r=[al.key], w=[al.key])
        S.op("act", act(al[0:R, :], al[0:R, :], AF.Ln, bias=one_t[0:R, 0:1]), r=[al.key, one_t.key], w=[al.key])
        S.op("dve", tt(g[0:R, :], al[0:R, :], negA[0:R, lo:hi], ALU.mult), r=[al.key, negA.key], w=[g.key])
        S.op("dve", tt(Gtri[0:R, :, 0:C], triu[0:R, :, 0:C], bc(g[0:R, :], C), ALU.mult),
             r=[triu.key, g.key] + mhi, w=[Gtri.key])
        for i_, po in enumerate(pos):
            S.op("pe", mmt(pv(GB[i_], C), ones_f[po:po + C, :], Gtri[po:po + C, :, 0:C], (po, 0)),
                 r=[ones_f.key, Gtri.key], w=[GB[i_].key])
        S.op("pe", [mmt(PA[po:po + C, 16:24], triu[po:po + C, 0, 0:C], g[po:po + C, :], (po, po)) for po in pos],
             r=[triu.key, g.key] + mhi, w=[PA.key])
        S.op("act", act(gc[0:R, :], PA[0:R, 16:24], AF.Identity), r=[PA.key], w=[gc.key])
        for i_, po in enumerate(pos):
            gv = pv(GB[i_], C)
            S.op("act", act(EGs[i_][:, :, 0:C], gv, AF.Exp), r=[GB[i_].key], w=[EGs[i_].key])
            S.op("dve", tt(Dm[po:po + C, :, 0:C], gc[po:po + C, :].unsqueeze(2).to_broadcast([C, 8, C]), gv[po:po + C],
                           ALU.subtract), r=[gc.key, GB[i_].key], w=[Dm.key])
            S.op("dve", tt(tcs[po:po + C, :], gv[po:po + C, :, C - 1], gc[po:po + C, :], ALU.subtract),
                 r=[GB[i_].key, gc.key], w=[tcs.key])
        S.op("dve", ts(dL[0:R, :, 0:C], Dm[0:R, :, 0:C], 0.0, ALU.min), r=[Dm.key], w=[dL.key])
        S.op("act", act(dL[0:R, :, 0:C], dL[0:R, :, 0:C], AF.Exp), r=[dL.key], w=[dL.key])
        S.op("dve", tt(dL[0:R, :, 0:C], dL[0:R, :, 0:C], maskL[0:R, :, 0:C], ALU.mult), r=[dL.key, maskL.key] + mhi, w=[dL.key])
        S.op("dve", ts(dU[0:R, :, 0:C], Dm[0:R, :, 0:C], -1.0, ALU.mult, 0.0, ALU.min), r=[Dm.key], w=[dU.key])
        S.op("act", act(dU[0:R, :, 0:C], dU[0:R, :, 0:C], AF.Exp), r=[dU.key], w=[dU.key])
        S.op("dve", tt(dU[0:R, :, 0:C], dU[0:R, :, 0:C], maskU[0:R, :, 0:C], ALU.mult), r=[dU.key, maskU.key] + mhi, w=[dU.key])
        S.op("act", act(egc[0:R, :], gc[0:R, :], AF.Exp), r=[gc.key], w=[egc.key])
        S.op("dve", tt(bge[0:R, :], bta[0:R, :], egc[0:R, :], ALU.mult), r=[bta.key, egc.key], w=[bge.key])
        S.op("act", act(edl[0:R, :], tcs[0:R, :], AF.Exp), r=[tcs.key], w=[edl.key])
        S.op("dve", ts(nb[0:R, :], bta[0:R, :], -1.0, ALU.mult), r=[bta.key], w=[nb.key])
        PCv = pv(PC, C)
        fl = []
        for (seq, off, _), po in zip(chs, pos):
            fl += [mmt(PCv[po:po + C, h, :], AR[:, 8 + h, off:off + C], AR[:, 8 + h, off:off + C], (0, po)) for h in range(8)]
        S.op("pe", fl, r=kk, w=[PC.key])
        A0 = Am[0]
        S.op("dve", tt(Dm[0:R, :, 0:C], PCv[0:R], bc(nb[0:R, :], C), ALU.mult), r=[PC.key, nb.key, Dm.key], w=[Dm.key])
        S.op("dve", tt(A0[0:R, :, 0:C], Dm[0:R, :, 0:C], dL[0:R, :, 0:C], ALU.mult), r=[Dm.key, dL.key], w=[A0.key])
        PAb = PA.t[:, :].bitcast(BF16)[:, 0:8 * C].rearrange("p (h c) -> p h c", c=C)
        fl = []
        for po in pos:
            fl += [trt(PAb[po:po + C, h, :], A0[po:po + C, h, 0:C], ident_b[po:po + C, po:po + C], (po, po)) for h in range(8)]
        S.op("pe", fl, r=[A0.key, ident_b.key], w=[PA.key])
        S.op("act", act(UX[0][0:R, :, 0:C], PAb[0:R], AF.Identity), r=[PA.key], w=[UX[0].key])
        S.op("dve", cp(UX[0][0:R, :, C:2 * C], identX[0:R, :, 0:C]), r=[identX.key, UX[0].key] + mhi, w=[UX[0].key])
        PEv = pv(PE2, C, 2 * C)

        def gen_neumann():
            cur = 0
            for k in range(nlev):
                Ac, UXc, An, UXn = Am[cur], UX[cur], Am[1 - cur], UX[1 - cur]
                if k < nlev - 1:
                    fl = []
                    for po in pos:
                        fl += [mmt(PEv[po:po + C, h, :], Ac[po:po + C, h, 0:C], UXc[po:po + C, h, 0:2 * C], (po, po)) for h in range(8)]
                    S.op("pe", fl, r=[Ac.key, UXc.key], w=[PE2.key]); yield
                    fl = []
                    for po in pos:
                        fl += [mmt(PCv[po:po + C, h, :], UXc[po:po + C, h, 0:C], Ac[po:po + C, h, 0:C], (po, po)) for h in range(8)]
                    S.op("pe", fl, r=[Ac.key, UXc.key], w=[PC.key]); yield
                    S.op("act", act(UXn[0:R, :, 0:C], PEv[0:R, :, 0:C], AF.Identity), r=[PE2.key], w=[UXn.key]); yield
                    S.op("dve", tt(UXn[0:R, :, C:2 * C], PEv[0:R, :, C:2 * C], UXc[0:R, :, C:2 * C], ALU.add),
                         r=[PE2.key, UXc.key, UXn.key], w=[UXn.key]); yield
                    S.op("act", act(An[0:R, :, 0:C], PCv[0:R], AF.Identity), r=[PC.key], w=[An.key]); yield
                else:
                    fl = []
                    for po in pos:
                        fl += [mmt(PEv[po:po + C, h, C:2 * C], Ac[po:po + C, h, 0:C], UXc[po:po + C, h, C:2 * C], (po, po)) for h in range(8)]
                    S.op("pe", fl, r=[Ac.key, UXc.key], w=[PE2.key]); yield
                    S.op("dve", tt(Xb[0:R, :, 0:C], PEv[0:R, :, C:2 * C], UXc[0:R, :, C:2 * C], ALU.add),
                         r=[PE2.key, UXc.key], w=[Xb.key]); yield
                cur = 1 - cur

        PFb = PF2.t[:, :].bitcast(BF16).rearrange("p (h d) -> p h d", d=128)
        PDq = pv(PD, C)

        def gen_other():
            fl = []
            for (seq, off, _), po in zip(chs, pos):
                fl += [trt(PFb[po:po + C, h, :], AR[:, 8 + h, off:off + C], ident_b[:, :], (0, po)) for h in range(8)]
                fl += [trt(PFb[po:po + C, 8 + h, :], AR[:, 16 + h, off:off + C], ident_b[:, :], (0, po)) for h in range(8)]
            S.op("pe", fl, r=kk + vk + [ident_b.key], w=[PF2.key]); yield
            S.op("dve", tt(vb[0:R, :, :], PFb[0:R, 8:16, :], bc(bta[0:R, :], 128), ALU.mult), r=[PF2.key, bta.key], w=[vb.key]); yield
            S.op("dve", tt(kbg[0:R, :, :], PFb[0:R, 0:8, :], bc(bge[0:R, :], 128), ALU.mult), r=[PF2.key, bge.key], w=[kbg.key]); yield
            S.op("dve", tt(kd[0:R, :, :], PFb[0:R, 0:8, :], bc(edl[0:R, :], 128), ALU.mult), r=[PF2.key, edl.key], w=[kd.key]); yield
            fl = []
            for (seq, off, _), po in zip(chs, pos):
                fl += [mmt(PDq[po:po + C, h, :], AR[:, 8 + h, off:off + C], AR[:, h, off:off + C], (0, po)) for h in range(8)]
            S.op("pe", fl, r=kk + qk, w=[PD.key]); yield
            S.op("dve", tt(aT[0:R, :, 0:C], PDq[0:R], dU[0:R, :, 0:C], ALU.mult), r=[PD.key, dU.key], w=[aT.key]); yield
            for i_, ((seq, off, _), po) in enumerate(zip(chs, pos)):
                S.op("dve", tt(qds[i_][:, :, 0:C], AR[:, 0:8, off:off + C], EGs[i_][:, :, 0:C], ALU.mult),
                     r=qk + [EGs[i_].key], w=[qds[i_].key]); yield

        gens = [gen_neumann(), gen_other()]
        while gens:
            for g_ in list(gens):
                try:
                    next(g_)
                except StopIteration:
                    gens.remove(g_)
        WB = [PA, PB]
        for i_, po in enumerate(pos):
            wv = pv(WB[i_], C)
            S.op("pe", [mmt(wv[:, h, :], kbg[po:po + C, h, :], Xb[po:po + C, h, 0:C], (po, 0)) for h in range(8)],
                 r=[kbg.key, Xb.key], w=[WB[i_].key])
            S.op("act", act(wTns[i_][:, :, 0:C], wv, AF.Identity, scale=-1.0), r=[WB[i_].key], w=[wTns[i_].key])
        PEw = PE2.t[:, :].rearrange("p (h d) -> p h d", d=128)
        PFs = PF2.t[:, :].rearrange("p (h d) -> p h d", d=128)
        PBv = pv(PB, C)
        for i_, ((seq, off, _), po) in enumerate(zip(chs, pos)):
            ST = ST_of(l, seq)
            SBs = SBb[seq]
            wTn, qd, EG = wTns[i_], qds[i_], EGs[i_]
            fl = []
            for h in range(8):
                fl.append(mmt(PEw[po:po + C, h, :], Xb[po:po + C, h, 0:C], vb[po:po + C, h, :], (po, po), True, False))
                fl.append(mmt(PEw[po:po + C, h, :], wTn[:, h, 0:C], SBs[:, h, :], (0, po), False, True))
            S.op("pe", fl, r=[Xb.key, vb.key, wTn.key, SBs.key], w=[PE2.key])
            S.op("act", act(vnb[po:po + C, :, :], PEw[po:po + C], AF.Identity), r=[PE2.key], w=[vnb.key])
            fl = []
            for h in range(8):
                fl.append(mmt(PBv[:, h, :], SBs[:, h, :], qd[:, h, 0:C], (0, 0), True, False))
                fl.append(mmt(PBv[:, h, :], vnb[po:po + C, h, :], aT[po:po + C, h, 0:C], (po, 0), False, True))
            S.op("pe", fl, r=[SBs.key, qd.key, vnb.key, aT.key], w=[PB.key])
            S.op("act", act(ofm[:, :, off:off + C], PBv, AF.Identity), r=[PB.key], w=[ofm.key])
            S.op("pe", [mmt(PFs[:, h, :], kd[po:po + C, h, :], vnb[po:po + C, h, :], (po, 0)) for h in range(8)],
                 r=[kd.key, vnb.key], w=[PF2.key])
            S.op("dve", tt(ST[:, :, :], ST[:, :, :], EG[:, :, C - 1:C].to_broadcast([128, 8, 128]), ALU.mult),
                 r=[ST.key, EG.key], w=[ST.key])
            S.op("dve", tt(ST[:, :, :], ST[:, :, :], PFs, ALU.add), r=[ST.key, PF2.key], w=[ST.key])
            S.op("act", act(SBs[:, :, :], ST[:, :, :], AF.Identity), r=[ST.key], w=[SBs.key])

    def mixer(l, blk):
        T = blk["T"]
        vi_mix = l * 7 + 1
        norm_to_xn(T, vi_mix)
        win = w_in[l].rearrange("(dc p) c -> p dc c", p=128)
        if blk["last"]:
            load_sample_state(l)
        S.phase = 'qkv'
        pendM = None
        pendS = None
        PSB = [PA, PB]
        PCB = [PC, PD]

        def p1_flush_silu():
            S.op("act", act(AR[:, pendS[0], :T], pendS[1][:, :T], AF.Silu), r=[pendS[1].key], w=[ark(pendS[0])])

        for tile_i in range(6):
            slot, sk = wload(win[:, :, tile_i * 512:(tile_i + 1) * 512], [128, 8, 512])
            for sub in range(4):
                cidx = tile_i * 4 + sub
                ps = proj(slot, sk, sub, T, PSB[cidx % 2])
                st, dg = conv_prep(l, cidx, ps, blk)
                if pendM is not None:
                    pc_ = conv_mm(pendM[1], pendM[2], blk, PCB[pendM[0] % 2])
                    if pendS is not None:
                        p1_flush_silu()
                    pendS = (pendM[0], pc_)
                pendM = (cidx, st, dg)
        pc_ = conv_mm(pendM[1], pendM[2], blk, PCB[pendM[0] % 2])
        p1_flush_silu()
        pendS = (pendM[0], pc_)
        p1_flush_silu()

        def qk_Y(ctx):
            cidx, psn = ctx
            ln = tmp()
            S.op("act", act(ln[:, :T], psn[:, :T], AF.Ln, bias=eps_t[:, 0:1]), r=[psn.key, eps_t.key], w=[ln.key])
            if cidx < 8:
                S.op("act", act(ln[:, :T], ln[:, :T], AF.Exp, scale=-0.5, bias=lnq_t[:, 0:1]),
                     r=[ln.key, lnq_t.key], w=[ln.key])
            else:
                S.op("act", act(ln[:, :T], ln[:, :T], AF.Exp, scale=-0.5), r=[ln.key], w=[ln.key])
            S.op("dve", tt(AR[:, cidx, :T], AR[:, cidx, :T], ln[:, :T], ALU.mult), r=[ark(cidx), ln.key], w=[ark(cidx)])

        pendY = None
        for cidx in range(16):
            sq = tmp()
            S.op("dve", tt(bfv(sq)[:, :T], AR[:, cidx, :T], AR[:, cidx, :T], ALU.mult), r=[ark(cidx)], w=[sq.key])
            psn = bank()
            S.op("pe", mm(psn[:, :T], ones_b[:, :], bfv(sq)[:, :T]), r=[sq.key, ones_b.key], w=[psn.key])
            if pendY is not None:
                qk_Y(pendY)
            pendY = (cidx, psn)
        qk_Y(pendY)
        stage('qkv')
        slot, sk = wload(win[:, :, OFF_BA:OFF_BA + 16], [128, 8, 16])
        S.op("act", act(wba[:, :, :], slot, AF.Identity), r=sk, w=[wba.key])
        S.phase = 'delta'
        started = set()
        segs = list(blk["segs"])
        groups = []
        i_ = 0
        while i_ < len(segs):
            if (i_ + 1 < len(segs) and segs[i_][2] == 64 and segs[i_ + 1][2] == 64 and segs[i_][0] == segs[i_ + 1][0]):
                groups.append([segs[i_], segs[i_ + 1]])
                i_ += 2
            else:
                groups.append([segs[i_]])
                i_ += 1
        for grp in groups:
            for (seq, off, C) in grp:
                if seq not in started:
                    started.add(seq)
                    ST = ST_of(l, seq)
                    S.op("act", act(SBb[seq][:, :, :], ST[:, :, :], AF.Identity), r=[ST.key], w=[SBb[seq].key])
            delta_pair(l, grp)
        stage('delta')
        S.phase = 'gnorm'
        def gn_B(ctx):
            h, zs, sq = ctx
            psn = bank()
            S.op("pe", mm(psn[:, :T], ones_b[:, :], bfv(sq)[:, :T]), r=[sq.key, ones_b.key], w=[psn.key]); yield
            ln = tmp()
            S.op("act", act(ln[:, :T], psn[:, :T], AF.Ln, bias=eps_t[:, 0:1], scale=1.0 / 128),
                 r=[psn.key, eps_t.key], w=[ln.key]); yield
            rs = ln
            S.op("act", act(rs[:, :T], ln[:, :T], AF.Exp, scale=-0.5), r=[ln.key], w=[rs.key]); yield
            t = tmp()
            S.op("dve", stt(t[:, :T], ofm[:, h, :T], onw(l), rs[:, :T], ALU.mult, ALU.mult),
                 r=[ofm.key, rs.key, VP.key], w=[t.key]); yield
            S.op("dve", tt(AR[:, 8 + h, :T], t[:, :T], zs[:, :T], ALU.mult), r=[t.key, zs.key], w=[ark(8 + h)]); yield

        def gn_A(h, slot, sk, sub, out):
            psz = proj(slot, sk, sub, T); yield
            zs = tmp()
            S.op("act", act(zs[:, :T], psz[:, :T], AF.Silu), r=[psz.key], w=[zs.key]); yield
            sq = tmp()
            S.op("act", act(bfv(sq)[:, :T], ofm[:, h, :T], AF.Square), r=[ofm.key], w=[sq.key]); yield
            out.append((h, zs, sq))

        def gn_interleave(ga, gb):
            live = [g_ for g_ in (ga, gb) if g_ is not None]
            while live:
                for g_ in list(live):
                    try:
                        next(g_)
                    except StopIteration:
                        live.remove(g_)

        pend = None
        for tile_i in range(2):
            slot, sk = wload(win[:, :, OFF_Z + tile_i * 512:OFF_Z + (tile_i + 1) * 512], [128, 8, 512])
            for sub in range(4):
                h = tile_i * 4 + sub
                out = []
                gn_interleave(gn_A(h, slot, sk, sub, out), gn_B(pend) if pend is not None else None)
                pend = out[0]
        gn_interleave(gn_B(pend), None)
        stage('gnorm')
        S.phase = 'lru'
        slot, sk = wload(lru_w_r[l].rearrange("n c d -> c n d"), [128, 8, 128])
        S.op("act", act(wri[:, 0:8, :], slot, AF.Identity), r=sk, w=[wri.key])
        slot, sk = wload(lru_w_i[l].rearrange("n c d -> c n d"), [128, 8, 128])
        S.op("act", act(wri[:, 8:16, :], slot, AF.Identity), r=sk, w=[wri.key])
        vb_ = l * 7 + 3
        PSR = Buf(PE2.t[:, 0:512], PE2.key)
        PSI = Buf(PF2.t[:, 0:512], PF2.key)
        PCB = [PC, PD]

        def lru_B(ctx):
            n_, xc, xcb, xl = ctx
            psr = PSR
            S.op("pe", mm(psr[:, :T], wri[:, n_, :], xcb[:, :T]), r=[wri.key, xcb.key], w=[psr.key]); yield
            rr = tmp()
            S.op("act", act(rr[:, :T], psr[:, :T], AF.Sigmoid, bias=vcol(vb_ + 1, n_)), r=[psr.key, VP.key], w=[rr.key]); yield
            psi = PSI
            S.op("pe", mm(psi[:, :T], wri[:, 8 + n_, :], xcb[:, :T]), r=[wri.key, xcb.key], w=[psi.key]); yield
            ii = tmp()
            S.op("act", act(ii[:, :T], psi[:, :T], AF.Sigmoid, bias=vcol(vb_ + 2, n_)), r=[psi.key, VP.key], w=[ii.key]); yield
            S.op("dve", tt(ii[:, :T], ii[:, :T], xc[:, :T], ALU.mult), r=[ii.key, xc.key], w=[ii.key]); yield
            aa = rr
            S.op("act", act(aa[:, :T], rr[:, :T], AF.Exp, scale=c8[:, l * 8 + n_:l * 8 + n_ + 1]),
                 r=[rr.key, c8.key], w=[aa.key]); yield
            a2 = tmp()
            S.op("dve", tt(a2[:, :T], aa[:, :T], aa[:, :T], ALU.mult), r=[aa.key], w=[a2.key]); yield
            S.op("act", act(a2[:, :T], a2[:, :T], AF.Sqrt, bias=one_t[:, 0:1], scale=-1.0), r=[a2.key, one_t.key], w=[a2.key]); yield
            S.op("dve", tt(ii[:, :T], ii[:, :T], a2[:, :T], ALU.mult), r=[ii.key, a2.key], w=[ii.key]); yield
            hh = hhs[hh_i[0] % 2]
            hh_i[0] += 1
            for (seq, c0, n, p0) in blk["runs"]:
                hst = hst_of(l, seq)
                S.op("dve", lambda e, c0=c0, n=n, hst=hst: e.tensor_tensor_scan(
                    out=hh[:, c0:c0 + n], data0=aa[:, c0:c0 + n], data1=ii[:, c0:c0 + n],
                    initial=hst[:, n_:n_ + 1], op0=ALU.mult, op1=ALU.add),
                    r=[aa.key, ii.key, hst.key], w=[hh.key]); yield
                S.op("dve", cp(hst[:, n_:n_ + 1], hh[:, c0 + n - 1:c0 + n]), r=[hh.key], w=[hst.key])
            S.op("dve", tt(AR[:, 16 + n_, :T], hh[:, :T], xl[:, :T], ALU.mult), r=[hh.key, xl.key], w=[ark(16 + n_)]); yield

        def lru_A(n_, slx, kx, sly, ky, sub, out):
            ps = proj(slx, kx, sub, T, PA); yield
            st, dg = conv_prep(l, 24 + n_, ps, blk); yield
            psy = proj(sly, ky, sub, T, PB); yield
            pc_ = conv_mm(st, dg, blk, PCB[n_ % 2]); yield
            xl = tmp()
            S.op("act", act(xl[:, :T], psy[:, :T], AF.Identity), r=[psy.key], w=[xl.key]); yield
            x2 = tmp()
            S.op("act", act(x2[:, :T], psy[:, :T], AF.Square), r=[psy.key], w=[x2.key]); yield
            xc = tmp()
            S.op("act", act(xc[:, :T], pc_[:, :T], AF.Identity, bias=vcol(vb_, n_)), r=[pc_.key, VP.key], w=[xc.key]); yield
            S.op("dve", ts(x2[:, :T], x2[:, :T], 0.044715, ALU.mult, 1.0, ALU.add), r=[x2.key], w=[x2.key]); yield
            xcb = tmpbf()
            S.op("dve", cp(xcb[:, :T], xc[:, :T]), r=[xc.key], w=[xcb.key]); yield
            S.op("dve", tt(x2[:, :T], x2[:, :T], xl[:, :T], ALU.mult), r=[x2.key, xl.key], w=[x2.key]); yield
            S.op("act", act(x2[:, :T], x2[:, :T], AF.Sigmoid, scale=1.5957691216057308), r=[x2.key], w=[x2.key]); yield
            S.op("dve", tt(xl[:, :T], xl[:, :T], x2[:, :T], ALU.mult), r=[xl.key, x2.key], w=[xl.key]); yield
            out.append((n_, xc, xcb, xl))

        def interleave(ga, gb):
            live = [g_ for g_ in (ga, gb) if g_ is not None]
            while live:
                for g_ in list(live):
                    try:
                        next(g_)
                    except StopIteration:
                        live.remove(g_)

        pend = None
        for tile_i in range(2):
            slx, kx = wload(win[:, :, OFF_LX + tile_i * 512:OFF_LX + (tile_i + 1) * 512], [128, 8, 512])
            sly, ky = wload(win[:, :, OFF_LY + tile_i * 512:OFF_LY + (tile_i + 1) * 512], [128, 8, 512])
            for sub in range(4):
                n_ = tile_i * 4 + sub
                out = []
                interleave(lru_A(n_, slx, kx, sly, ky, sub, out), lru_B(pend) if pend is not None else None)
                pend = out[0]
        interleave(lru_B(pend), None)
        stage('lru')
        S.phase = 'merge'
        wa = w_branch_a[l].rearrange("(dc p) c -> p dc c", p=128)
        wb_ = w_branch_b[l].rearrange("(dc p) c -> p dc c", p=128)
        for tile_i in range(2):
            sa, ka = wload(wa[:, :, tile_i * 512:(tile_i + 1) * 512], [128, 8, 512])
            sga_, kga = wload(win[:, :, OFF_GA + tile_i * 512:OFF_GA + (tile_i + 1) * 512], [128, 8, 512])
            for sub in range(4):
                m = tile_i * 4 + sub
                psa = bank()
                S.op("pe", [mm(psa[:, :T], sa[:, h, sub * 128:(sub + 1) * 128], AR[:, 8 + h, :T], h == 0, h == 7)
                            for h in range(8)], r=ka + [ark(8 + h) for h in range(8)], w=[psa.key])
                psg = proj(sga_, kga, sub, T)
                sg = tmp()
                S.op("act", act(sg[:, :T], psg[:, :T], AF.Sigmoid), r=[psg.key], w=[sg.key])
                S.op("dve", tt(ofm[:, m, :T], sg[:, :T], psa[:, :T], ALU.mult), r=[sg.key, psa.key], w=[ofm.key])
            sb_, kb = wload(wb_[:, :, tile_i * 512:(tile_i + 1) * 512], [128, 8, 512])
            sgb_, kgb = wload(win[:, :, OFF_GB + tile_i * 512:OFF_GB + (tile_i + 1) * 512], [128, 8, 512])
            for sub in range(4):
                m = tile_i * 4 + sub
                psb = bank()
                S.op("pe", [mm(psb[:, :T], sb_[:, h, sub * 128:(sub + 1) * 128], AR[:, 16 + h, :T], h == 0, h == 7)
                            for h in range(8)], r=kb + [ark(16 + h) for h in range(8)], w=[psb.key])
                psg = proj(sgb_, kgb, sub, T)
                sg = tmp()
                S.op("act", act(sg[:, :T], psg[:, :T], AF.Sigmoid), r=[psg.key], w=[sg.key])
                t = tmp()
                S.op("dve", tt(t[:, :T], sg[:, :T], psb[:, :T], ALU.mult), r=[sg.key, psb.key], w=[t.key])
                S.op("dve", tt(AR[:, m, :T], t[:, :T], ofm[:, m, :T], ALU.add), r=[t.key, ofm.key], w=[ark(m)])
        stage('merge')
        S.phase = 'wout'
        wo = w_out[l].rearrange("(dc p) c -> p dc c", p=128)
        for tile_i in range(2):
            so, ko = wload(wo[:, :, tile_i * 512:(tile_i + 1) * 512], [128, 8, 512])
            for sub in range(4):
                m = tile_i * 4 + sub
                pso = bank()
                S.op("pe", [mm(pso[:, :T], so[:, h, sub * 128:(sub + 1) * 128], AR[:, h, :T], h == 0, h == 7)
                            for h in range(8)], r=ko + [ark(h) for h in range(8)], w=[pso.key])
                S.op("dve", tt(x[:, m, :T], x[:, m, :T], pso[:, :T], ALU.add), r=[xkeys[m], pso.key], w=[xkeys[m]])
        S.phase = 'tails'
        if blk["last"]:
            for seq in (0, 1, 2):
                write_tails(l, seq)

    blocks = make_blocks()
    srcmap = {"meta": meta, "xp": xp, "xs": xs}
    try:
      stage('const')
      for b in range(5):
        if blocks_sel is not None and b not in blocks_sel:
            continue
        blk = blocks[b]
        T = blk["T"]
        wl_blk[0] = 0
        wl_pass[0] = 0 if nblk_done[0] == 0 else 1
        nblk_done[0] += 1
        S.phase = 'load'
        rows = []
        for (nm, r0, n) in blk["srcs"]:
            rows += [(nm, r0 + i) for i in range(n)]
        ntile = (T + 127) // 128
        for ti in range(ntile):
            c0 = ti * 128
            n = min(128, T - c0)
            i = 0
            while i < n:
                nm, r0 = rows[c0 + i]
                j = i
                while j + 1 < n and rows[c0 + j + 1] == (nm, r0 + (j + 1 - i)):
                    j += 1
                cnt = j - i + 1
                S.dma("sp", xtm_in[i:i + cnt, :], srcmap[nm][r0:r0 + cnt, :], w=[xtm_in.key], skey="xtm_in")
                i = j + 1
            for half in range(2):
                Pt = PE2 if half == 0 else PF2
                S.op("pe", [tr(Pt[:, q * 128:q * 128 + n], xtm_in[0:n, (half * 4 + q) * 128:(half * 4 + q + 1) * 128],
                               ident_f[0:n, 0:n]) for q in range(4)], r=[xtm_in.key, ident_f.key], w=[Pt.key])
                for q in range(4):
                    dc = half * 4 + q
                    S.op("act" if q % 2 else "dve",
                         (act(x[:, dc, c0:c0 + n], Pt[:, q * 128:q * 128 + n], AF.Identity) if q % 2 else
                          cp(x[:, dc, c0:c0 + n], Pt[:, q * 128:q * 128 + n])),
                         r=[Pt.key], w=[xkeys[dc]])
        stage('load')
        for l in range(nl):
            ffn(l, T, l * 7 + 0, ffn1_w_gu, ffn1_w_down)
            stage('ffn1')
            mixer(l, blk)
            stage('mixer')
            ffn(l, T, l * 7 + 2, ffn2_w_gu, ffn2_w_down)
            stage('ffn2')
        S.phase = 'final'
        rs = rms_stats(T, lambda dc: x[:, dc, :T], xkeys, 8, 1.0 / D)
        for dc in range(8):
            S.op("dve", stt(ofm[:, dc, :T], x[:, dc, :T], vcol(28, dc), rs[:, :T], ALU.mult, ALU.mult),
                 r=[xkeys[dc], rs.key, VP.key], w=[ofm.key])
        dst = []
        if b == 0:
            dst += [None] * 16 + [("y_p", i) for i in range(T - 16)]
        elif b < 4:
            dst += [("y_p", blk["p0"] - 16 + i) for i in range(T)]
        else:
            dst += [("y_p", blk["p0"] - 16 + i) for i in range(T - 64)] + [("y_s", i) for i in range(64)]
        dmap = {"y_p": y_p, "y_s": y_s}
        for ti in range(ntile):
            c0 = ti * 128
            n = min(128, T - c0)
            for half in range(2):
                Pt = PE2 if half == 0 else PF2
                S.op("pe", [tr(Pt[0:n, q * 128:(q + 1) * 128], ofm[:, half * 4 + q, c0:c0 + n], ident_f[:, :])
                            for q in range(4)], r=[ofm.key, ident_f.key], w=[Pt.key])
                S.op("act" if half else "dve",
                     (act(xtm_out[0:n, half * 512:(half + 1) * 512], Pt[0:n, 0:512], AF.Identity) if half else
                      cp(xtm_out[0:n, half * 512:(half + 1) * 512], Pt[0:n, 0:512])),
                     r=[Pt.key], w=[xtm_out.key])
            i = 0
            while i < n:
                if dst[c0 + i] is None:
                    i += 1
                    continue
                nm, r0 = dst[c0 + i]
                j = i
                while j + 1 < n and dst[c0 + j + 1] == (nm, r0 + (j + 1 - i)):
                    j += 1
                cnt = j - i + 1
                S.dma("sp", dmap[nm][r0:r0 + cnt, :], xtm_out[i:i + cnt, :], r=[xtm_out.key], skey="xtm_out")
                i = j + 1
    except _Stop:
        pass
    S.finish()
    return nc, es, S


_CACHE = {}


def kernel(**inputs):
    nl = NL_FULL
    if "prog" not in _CACHE:
        _CACHE["prog"] = build(nl)
    nc, es, S = _CACHE["prog"]
    f = lambda a: np.ascontiguousarray(np.asarray(a, dtype=np.float32))
    wnames = ["ffn1_norm", "ffn1_w_gu", "ffn1_w_down", "mix_norm", "w_in", "delta_conv_w", "delta_A_log",
              "delta_dt_bias", "delta_out_norm", "lru_conv_w", "lru_conv_b", "lru_w_r", "lru_b_r", "lru_w_i",
              "lru_b_i", "lru_lambda", "w_branch_a", "w_branch_b", "w_out", "ffn2_norm", "ffn2_w_gu", "ffn2_w_down"]
    shared = {n: f(inputs[n]) for n in wnames}
    shared["meta"] = f(inputs["meta_tokens"])
    shared["final_norm"] = f(inputs["final_norm"]).reshape(1, D)
    x_prompt = f(inputs["x_prompt"])
    x_sample = f(inputs["x_sample"])
    sS = f(inputs["state_delta_S"])
    scq = f(inputs["state_delta_conv"])
    sh = f(inputs["state_lru_h"])
    scx = f(inputs["state_lru_conv"])
    in_maps = []
    for c in range(8):
        m = dict(shared)
        m["xp"] = x_prompt[c]
        m["xs"] = np.ascontiguousarray(x_sample[2 * c:2 * c + 2].reshape(64, D))
        m["st_S"] = np.ascontiguousarray(sS[:, 2 * c:2 * c + 2])
        m["st_cq"] = np.ascontiguousarray(scq[:, 2 * c:2 * c + 2])
        m["st_h"] = np.ascontiguousarray(sh[:, 2 * c:2 * c + 2])
        m["st_cx"] = np.ascontiguousarray(scx[:, 2 * c:2 * c + 2])
        in_maps.append(m)
    res = run_bass_kernel_spmd(nc, in_maps, core_ids=list(range(8)))
    R = res.results
    y_prompt = np.stack([R[c]["y_p"] for c in range(8)], 0)
    y_sample = np.concatenate([R[c]["y_s"].reshape(2, 32, D) for c in range(8)], 0)
    p_S = np.stack([R[c]["o_pS"] for c in range(8)], 1)
    p_cq = np.stack([R[c]["o_pcq"] for c in range(8)], 1)
    p_h = np.stack([R[c]["o_ph"] for c in range(8)], 1)
    p_cx = np.stack([R[c]["o_pcx"] for c in range(8)], 1)
    s_S = np.concatenate([R[c]["o_sS"] for c in range(8)], 1)
    s_cq = np.concatenate([R[c]["o_scq"] for c in range(8)], 1)
    s_h = np.concatenate([R[c]["o_sh"] for c in range(8)], 1)
    s_cx = np.concatenate([R[c]["o_scx"] for c in range(8)], 1)
    outs = (y_prompt, y_sample, p_S, p_cq, p_h, p_cx, s_S, s_cq, s_h, s_cx)
    return tuple(np.ascontiguousarray(o.astype(np.float32)) for o in outs)
```

```python
import contextlib
import math
import numpy as np
import concourse.bass as bass
import concourse.mybir as mybir
from concourse.bass_utils import run_bass_kernel_spmd

F32 = mybir.dt.float32
BF16 = mybir.dt.bfloat16
AF = mybir.ActivationFunctionType
ALU = mybir.AluOpType

D = 1024
NL_FULL = 4
SEQ = 2048
NMETA = 16
DFF = 2816
NFC = 22
QKV = 3072
OFF_Z = 3072
OFF_BA = 4096
OFF_LX = 4112
OFF_LY = OFF_LX + 1024
OFF_GA = OFF_LY + 1024
OFF_GB = OFF_GA + 1024
INC = OFF_GB + 1024
EPS = 1e-6
TMAX = 448
SAME_ENGINE_SYNC = True
SEM_ROT = 30000


class Buf:
    def __init__(self, t, key):
        self.t = t
        self.key = key

    def __getitem__(self, idx):
        return self.t[idx]


class Sched:
    def __init__(self, nc, es):
        self.nc = nc
        self.es = es
        self.nsem = 0
        self.eng = {}
        for name, h in (("pe", nc.tensor), ("act", nc.scalar), ("dve", nc.vector),
                        ("pool", nc.gpsimd), ("sp", nc.sync)):
            self.eng[name] = dict(h=h, sem=None, cnt=0, seen={})
            self._new_sem(name)
        self.last_w = {}
        self.readers = {}
        self.dma_sems = {}
        self.ninst = 0
        self.excl = set()
        self.phase = 'init'
        self.pe_log = []

    def _mk(self, owner):
        nm = f"s{self.nsem}_{owner}"
        self.nsem += 1
        h = self.es.enter_context(self.nc.semaphore(nm))
        return (nm, h, owner)

    def _new_sem(self, name):
        e = self.eng[name]
        e["sem"] = self._mk(name)
        e["cnt"] = 0

    def _deps(self, r, w, ename=None):
        d = {}

        def add(tok):
            if tok is None:
                return
            k = tok[0][0]
            if k not in d or d[k][1] < tok[1]:
                d[k] = tok
        for k in r:
            add(self.last_w.get(k))
            if k in self.excl:
                for tok in self.readers.get(k, {}).values():
                    if tok[0][2] != ename:
                        add(tok)
        for k in w:
            add(self.last_w.get(k))
            for tok in self.readers.get(k, {}).values():
                add(tok)
        return d

    def _wait(self, ename, deps):
        e = self.eng[ename]
        for k, (s, v) in deps.items():
            if s[2] == ename and (ename == "pe" or not SAME_ENGINE_SYNC):
                continue
            if e["seen"].get(k, 0) >= v:
                continue
            e["h"].wait_ge(s[1], v)
            e["seen"][k] = v

    def _record(self, tok, r, w):
        for k in r:
            self.readers.setdefault(k, {})[tok[0][0]] = tok
        for k in w:
            self.last_w[k] = tok
            self.readers[k] = {}

    def op(self, ename, fns, r=(), w=()):
        self._wait(ename, self._deps(r, w, ename))
        e = self.eng[ename]
        if not isinstance(fns, (list, tuple)):
            fns = [fns]
        inst = None
        if ename == 'pe':
            self.pe_log.extend([self.phase] * len(fns))
        for f in fns:
            inst = f(e["h"])
            self.ninst += 1
        e["cnt"] += 1
        inst.then_inc(e["sem"][1], 1)
        tok = (e["sem"], e["cnt"])
        self._record(tok, r, w)
        if e["cnt"] >= SEM_ROT:
            self._new_sem(ename)

    def dma(self, qname, out, in_, r=(), w=(), skey=None):
        self._wait(qname, self._deps(r, w))
        if skey not in self.dma_sems:
            self.dma_sems[skey] = [self._mk("dma"), 0]
        ds = self.dma_sems[skey]
        if not isinstance(out, (list, tuple)):
            out, in_ = [out], [in_]
        for o_, i_ in zip(out, in_):
            ds[1] += 16
            self.eng[qname]["h"].dma_start(out=o_, in_=i_).then_inc(ds[0][1], 16)
            self.ninst += 1
        self._record((ds[0], ds[1]), r, w)

    def finish(self):
        sp = self.eng["sp"]["h"]
        for ds in self.dma_sems.values():
            sp.wait_ge(ds[0][1], ds[1])
        for name in ("pe", "act", "dve"):
            e = self.eng[name]
            if e["cnt"] > 0:
                sp.wait_ge(e["sem"][1], e["cnt"])


def make_blocks():
    blocks = []
    bounds = [0, 400, 848, 1296, 1744, 2064]
    for b in range(5):
        p0, p1 = bounds[b], bounds[b + 1]
        n = p1 - p0
        segs = []
        pos = p0
        if b == 0:
            segs.append((0, 0, 16))
            pos = 16
        while pos < p1:
            segs.append((0, pos - p0, 64))
            pos += 64
        srcs = []
        if b == 0:
            srcs.append(("meta", 0, 16))
            srcs.append(("xp", 0, n - 16))
        else:
            srcs.append(("xp", p0 - 16, n))
        runs = [(0, 0, n)]
        T = n
        if b == 4:
            runs.append((1, T, 32))
            segs.append((1, T, 32))
            runs.append((2, T + 32, 32))
            segs.append((2, T + 32, 32))
            srcs.append(("xs", 0, 64))
            T += 64
        pruns = []
        for i, (sq, c0, nn) in enumerate(runs):
            pruns.append((sq, c0, nn, c0 + 3 * i))
        blocks.append(dict(T=T, runs=pruns, segs=segs, srcs=srcs, p0=p0, last=(b == 4)))
    return blocks


class _Stop(Exception):
    pass


def build(nl=NL_FULL, blocks_sel=None, stop=None):
    nc = bass.Bass("TRN2", target_bir_lowering=False)
    es = contextlib.ExitStack()

    def din(name, shape):
        return nc.dram_tensor(name, list(shape), F32, kind="ExternalInput").ap()

    def dout(name, shape):
        return nc.dram_tensor(name, list(shape), F32, kind="ExternalOutput").ap()

    xp = din("xp", [SEQ, D])
    xs = din("xs", [64, D])
    st_S = din("st_S", [NL_FULL, 2, 8, 128, 128])
    st_cq = din("st_cq", [NL_FULL, 2, 3, QKV])
    st_h = din("st_h", [NL_FULL, 2, D])
    st_cx = din("st_cx", [NL_FULL, 2, 3, D])
    meta = din("meta", [NMETA, D])
    ffn1_norm = din("ffn1_norm", [NL_FULL, D])
    ffn1_w_gu = din("ffn1_w_gu", [NL_FULL, D, 2 * DFF])
    ffn1_w_down = din("ffn1_w_down", [NL_FULL, DFF, D])
    mix_norm = din("mix_norm", [NL_FULL, D])
    w_in = din("w_in", [NL_FULL, D, INC])
    delta_conv_w = din("delta_conv_w", [NL_FULL, 4, QKV])
    delta_A_log = din("delta_A_log", [NL_FULL, 8])
    delta_dt_bias = din("delta_dt_bias", [NL_FULL, 8])
    delta_out_norm = din("delta_out_norm", [NL_FULL, 128])
    lru_conv_w = din("lru_conv_w", [NL_FULL, 4, D])
    lru_conv_b = din("lru_conv_b", [NL_FULL, D])
    lru_w_r = din("lru_w_r", [NL_FULL, 8, 128, 128])
    lru_b_r = din("lru_b_r", [NL_FULL, D])
    lru_w_i = din("lru_w_i", [NL_FULL, 8, 128, 128])
    lru_b_i = din("lru_b_i", [NL_FULL, D])
    lru_lambda = din("lru_lambda", [NL_FULL, D])
    w_branch_a = din("w_branch_a", [NL_FULL, D, D])
    w_branch_b = din("w_branch_b", [NL_FULL, D, D])
    w_out = din("w_out", [NL_FULL, D, D])
    ffn2_norm = din("ffn2_norm", [NL_FULL, D])
    ffn2_w_gu = din("ffn2_w_gu", [NL_FULL, D, 2 * DFF])
    ffn2_w_down = din("ffn2_w_down", [NL_FULL, DFF, D])
    final_norm = din("final_norm", [1, D])

    y_p = dout("y_p", [SEQ, D])
    y_s = dout("y_s", [64, D])
    o_pS = dout("o_pS", [NL_FULL, 8, 128, 128])
    o_pcq = dout("o_pcq", [NL_FULL, 3, QKV])
    o_ph = dout("o_ph", [NL_FULL, D])
    o_pcx = dout("o_pcx", [NL_FULL, 3, D])
    o_sS = dout("o_sS", [NL_FULL, 2, 8, 128, 128])
    o_scq = dout("o_scq", [NL_FULL, 2, 3, QKV])
    o_sh = dout("o_sh", [NL_FULL, 2, D])
    o_scx = dout("o_scx", [NL_FULL, 2, 3, D])

    TPB = 63 * nl
    wscr = nc.dram_tensor("wscr", [TPB, 128, 4096], BF16, kind="Internal").ap()
    S = Sched(nc, es)

    def stage(name):
        if stop == name:
            raise _Stop()

    def sb(name, shape, dt=F32):
        return Buf(es.enter_context(nc.sbuf_tensor(name, list(shape), dt)), name)

    def pm(name, shape, dt=F32):
        S.excl.add(name)
        return Buf(es.enter_context(nc.psum_tensor(name, list(shape), dt)), name)

    def mm(out, lhsT, rhs, start=True, stop=True):
        return lambda e: e.matmul(out, lhsT=lhsT, rhs=rhs, start=start, stop=stop)

    def tr(out, in_, ident):
        return lambda e: e.transpose(out, in_, ident)

    def act(out, in_, func, bias=None, scale=None):
        kw = {}
        if bias is not None:
            kw["bias"] = bias
        if scale is not None:
            kw["scale"] = scale
        return lambda e: e.activation(out=out, in_=in_, func=func, **kw)

    def tt(out, in0, in1, op):
        return lambda e: e.tensor_tensor(out=out, in0=in0, in1=in1, op=op)

    def ts(out, in0, s1, op0, s2=None, op1=None):
        if op1 is None:
            return lambda e: e.tensor_scalar(out=out, in0=in0, scalar1=s1, scalar2=None, op0=op0)
        return lambda e: e.tensor_scalar(out=out, in0=in0, scalar1=s1, scalar2=s2, op0=op0, op1=op1)

    def stt(out, in0, scalar, in1, op0, op1):
        return lambda e: e.scalar_tensor_tensor(out=out, in0=in0, scalar=scalar, in1=in1, op0=op0, op1=op1)

    def cp(out, in_):
        return lambda e: e.tensor_copy(out=out, in_=in_)

    def mset(ap, v):
        return lambda e: e.memset(ap, v)

    ident_f = sb("ident_f", [128, 128])
    ident_b = sb("ident_b", [128, 128], BF16)
    ones_f = sb("ones_f", [128, 128])
    ones_b = sb("ones_b", [128, 128], BF16)
    eps_t = sb("eps_t", [128, 1])
    one_t = sb("one_t", [128, 1])
    lnq_t = sb("lnq_t", [128, 1])
    triu = sb("triu", [128, 8, 64])
    maskL = sb("maskL", [128, 8, 64])
    identX = sb("identX", [128, 8, 64])
    VP = sb("VP", [128, 32, 32])
    c8 = sb("c8", [128, 32])
    dtb = sb("dtb", [128, NL_FULL * 8])
    negA = sb("negA", [128, NL_FULL * 8])

    x = sb("x", [128, 8, TMAX])
    xn = sb("xn", [128, 8, TMAX], BF16)
    AR = sb("AR", [128, 24, TMAX], BF16)
    NT = 12
    tmps = [sb(f"tmp{i}", [128, TMAX]) for i in range(NT)]
    tmpb = [sb(f"tmpb{i}", [128, TMAX], BF16) for i in range(2)]
    stgs = [sb(f"stg{i}", [128, TMAX + 12], BF16) for i in range(2)]
    dgs = [sb(f"dg{i}", [128, 4, 128], BF16) for i in range(2)]
    dg_i = [0]
    hhs = [sb(f"hh{i}", [128, TMAX]) for i in range(2)]
    xtm_in = sb("xtm_in", [128, D])
    xtm_out = xtm_in
    ofm = sb("ofm", [128, 8, TMAX])
    wba = sb("wba", [128, 8, 16], BF16)
    wri = sb("wri", [128, 16, 128], BF16)
    RING = 4
    WSLOT = 4096
    ring = [sb(f"wr{i}", [128, WSLOT], BF16) for i in range(RING)]
    ST_P = [sb(f"STP{l}", [128, 8, 128]) for l in range(nl)]
    ST_S = [sb(f"STS{s}", [128, 8, 128]) for s in range(2)]
    SBb = [sb(f"SB{s}", [128, 8, 128], BF16) for s in range(3)]
    histP = [sb(f"histP{l}", [128, 32, 3]) for l in range(nl)]
    histS = [sb(f"histS{s}", [128, 32, 3]) for s in range(2)]
    hstP = [sb(f"hstP{l}", [128, 8]) for l in range(nl)]
    hstS = [sb(f"hstS{s}", [128, 8]) for s in range(2)]
    TT = sb("TT", [128, 104])
    TTt = sb("TTt", [128, 128])
    Dm = sb("Dm", [128, 8, 64])
    dL = sb("dL", [128, 8, 64])
    dU = sb("dU", [128, 8, 64])
    Gtri = dU
    Am = [sb(f"Am{i}", [128, 8, 64], BF16) for i in range(2)]
    UX = [sb(f"UX{i}", [128, 8, 128], BF16) for i in range(2)]
    EGs = [sb(f"EG{i}", [128, 8, 64]) for i in range(2)]
    Xb = sb("Xb", [128, 8, 64], BF16)
    aT = sb("aT", [128, 8, 64], BF16)
    vb = sb("vb", [128, 8, 128], BF16)
    kbg = sb("kbg", [128, 8, 128], BF16)
    kd = sb("kd", [128, 8, 128], BF16)
    wTns = [sb(f"wTn{i}", [128, 8, 64], BF16) for i in range(2)]
    vnb = sb("vnb", [128, 8, 128], BF16)
    qds = [sb(f"qd{i}", [128, 8, 64], BF16) for i in range(2)]
    scl = {n: sb(f"sc_{n}", [128, 8]) for n in ("beta", "al", "g", "gc", "nb", "egc", "bge", "edl", "tc")}

    PA = pm("PA", [128, 512])
    PB = pm("PB", [128, 512])
    PC = pm("PC", [128, 512])
    PD = pm("PD", [128, 512])
    PE2 = pm("PE2", [128, 1024])
    PF2 = pm("PF2", [128, 1024])
    banks = [PA, PB, PC, PD]
    bank_i = [0]

    def bank():
        b = banks[bank_i[0] % 4]
        bank_i[0] += 1
        return b

    tmp_i = [0]

    def tmp():
        t = tmps[tmp_i[0] % NT]
        tmp_i[0] += 1
        return t

    def bfv(t):
        return t.t[:, :].bitcast(BF16)

    tb_i = [0]

    def tmpbf():
        t = tmpb[tb_i[0] % 2]
        tb_i[0] += 1
        return t

    stg_i = [0]
    hh_i = [0]

    def pv(P, C, w=None):
        w = C if w is None else w
        return P.t[:, 0:8 * w].rearrange("p (h c) -> p h c", c=w)

    wl_n = [0]
    wl_blk = [0]
    wl_pass = [0]
    nblk_done = [0]

    def wload(dram_views, shape):
        i = wl_n[0] % RING
        wl_n[0] += 1
        slot = ring[i]
        n = 1
        for s_ in shape[1:]:
            n *= s_
        flat = slot.t[:, 0:n]
        keys = [(slot.key, 0), (slot.key, 1)]
        if len(shape) == 3:
            view = flat.rearrange("p (a b) -> p a b", b=shape[2])
        elif len(shape) == 4:
            view = flat.rearrange("p (a b c) -> p a b c", b=shape[2], c=shape[3])
        else:
            view = flat
        tid = wl_blk[0]
        wl_blk[0] += 1
        if wl_pass[0] == 0:
            if isinstance(dram_views, (list, tuple)):
                S.dma("pool", [view[:, :, g_, :] for g_ in range(len(dram_views))], list(dram_views), r=(), w=keys,
                      skey=slot.key)
            else:
                S.dma("pool", view, dram_views, r=(), w=keys, skey=slot.key)
            S.dma("sp", wscr[tid, :, 0:n], flat, r=keys, w=[("scr", tid)], skey=("scrst", i))
        else:
            S.dma("pool", flat, wscr[tid, :, 0:n], r=[("scr", tid)], w=keys, skey=slot.key)
        return view, keys

    S.op("dve", mset(ones_f[:, :], 1.0), w=[ones_f.key])
    S.op("dve", cp(ones_b[:, :], ones_f[:, :]), r=[ones_f.key], w=[ones_b.key])
    S.op("dve", mset(eps_t[:, :], EPS), w=[eps_t.key])
    S.op("dve", mset(one_t[:, :], 1.0), w=[one_t.key])
    S.op("dve", mset(lnq_t[:, :], -0.5 * math.log(128.0)), w=[lnq_t.key])
    S.op("pool", lambda e: e.affine_select(out=ident_f[:, :], in_=ones_f[:, :], pattern=[[-1, 128]],
                                            compare_op=ALU.is_equal, fill=0.0, base=0, channel_multiplier=1),
         r=[ones_f.key], w=[ident_f.key])
    S.op("dve", cp(ident_b[:, :], ident_f[:, :]), r=[ident_f.key], w=[ident_b.key])
    for h in range(8):
        S.op("pool", lambda e, h=h: e.affine_select(out=triu[0:64, h, :], in_=ones_f[0:64, 0:64], pattern=[[1, 64]],
                                                    compare_op=ALU.is_ge, fill=0.0, base=0, channel_multiplier=-1),
             r=[ones_f.key], w=[triu.key])
        S.op("pool", lambda e, h=h: e.affine_select(out=maskL[0:64, h, :], in_=ones_f[0:64, 0:64], pattern=[[-1, 64]],
                                                    compare_op=ALU.is_gt, fill=0.0, base=0, channel_multiplier=1),
             r=[ones_f.key], w=[maskL.key])
        S.op("pool", lambda e, h=h: e.affine_select(out=identX[0:64, h, :], in_=ones_f[0:64, 0:64], pattern=[[-1, 64]],
                                                    compare_op=ALU.is_equal, fill=0.0, base=0, channel_multiplier=1),
             r=[ones_f.key], w=[identX.key])
    for mk in (triu, maskL, identX):
        S.dma("sp", mk[64:128, :, :], mk[0:64, :, :], r=[mk.key], w=[(mk.key, 'hi')], skey=(mk.key, 'hi'))
    maskU = triu

    vecs = []
    for l in range(NL_FULL):
        vecs += [ffn1_norm[l:l + 1, :], mix_norm[l:l + 1, :], ffn2_norm[l:l + 1, :], lru_conv_b[l:l + 1, :],
                 lru_b_r[l:l + 1, :], lru_b_i[l:l + 1, :], lru_lambda[l:l + 1, :]]
    vecs.append(final_norm[0:1, :])

    def vcol(vi, dc):
        return VP[:, (vi % 4) * 8 + dc, 16 + vi // 4: 17 + vi // 4]

    def convw(l, cidx, j):
        return VP[:, cidx, l * 4 + j: l * 4 + j + 1]

    S.op("dve", mset(xtm_in[:, :], 0.0), w=[xtm_in.key])
    PVP = PE2.t[:, 0:1024].rearrange("p (c r) -> p c r", r=32)
    for g in range(4):
        for l in range(NL_FULL):
            if g < 3:
                src = delta_conv_w[l, :, g * 1024:(g + 1) * 1024]
            else:
                src = lru_conv_w[l, :, :]
            S.dma("sp", xtm_in[l * 4:(l + 1) * 4, :], src, w=[xtm_in.key], skey="xtm_in")
        for vi in range(g, len(vecs), 4):
            S.dma("sp", xtm_in[16 + vi // 4:17 + vi // 4, :], vecs[vi], w=[xtm_in.key], skey="xtm_in")
        if g == 0:
            S.dma("sp", xtm_in[24:25, 0:512], delta_out_norm.rearrange("l d -> (l d)").unsqueeze(0),
                  w=[xtm_in.key], skey="xtm_in")
        S.op("pe", [tr(PVP[:, 8 * g + c, 0:25], xtm_in[0:25, c * 128:(c + 1) * 128], ident_f[0:25, 0:25])
                    for c in range(8)], r=[xtm_in.key, ident_f.key], w=[PE2.key])
    S.op("dve", cp(VP[:, :, 0:25], PVP[:, :, 0:25]), r=[PE2.key], w=[VP.key])

    def onw(l):
        return VP[:, l, 24:25]

    for l in range(NL_FULL):
        vi = l * 7 + 6
        for dc in range(8):
            S.op("act", act(c8[:, l * 8 + dc:l * 8 + dc + 1], vcol(vi, dc), AF.Exp, scale=-1.0),
                 r=[VP.key], w=[c8.key])
    S.op("act", act(c8[:, :], c8[:, :], AF.Ln, bias=one_t[:, 0:1]), r=[c8.key, one_t.key], w=[c8.key])
    S.op("dve", ts(c8[:, :], c8[:, :], -8.0, ALU.mult), r=[c8.key], w=[c8.key])
    S.dma("sp", dtb[:, :], delta_dt_bias.rearrange("l h -> (l h)").partition_broadcast(128), w=[dtb.key], skey="dtb")
    S.dma("sp", negA[:, :], delta_A_log.rearrange("l h -> (l h)").partition_broadcast(128), w=[negA.key], skey="negA")
    S.op("act", act(negA[:, :], negA[:, :], AF.Exp), r=[negA.key], w=[negA.key])
    S.op("dve", ts(negA[:, :], negA[:, :], -1.0, ALU.mult), r=[negA.key], w=[negA.key])
    for l in range(nl):
        S.op("dve", mset(ST_P[l][:, :, :], 0.0), w=[ST_P[l].key])
        S.op("dve", mset(histP[l][:, :, :], 0.0), w=[histP[l].key])
        S.op("dve", mset(hstP[l][:, :], 0.0), w=[hstP[l].key])

    xkeys = [("x", dc) for dc in range(8)]
    xnkeys = [("xn", dc) for dc in range(8)]

    def ark(i):
        return ("AR", i)

    def rms_stats(T, src_fn, src_keys, nchunk, scale):
        ps = bank()
        for dc in range(nchunk):
            t = tmp()
            if dc % 2 == 0:
                S.op("act", act(bfv(t)[:, :T], src_fn(dc), AF.Square), r=[src_keys[dc]], w=[t.key])
            else:
                S.op("dve", tt(bfv(t)[:, :T], src_fn(dc), src_fn(dc), ALU.mult), r=[src_keys[dc]], w=[t.key])
            S.op("pe", mm(ps[:, :T], ones_b[:, :], bfv(t)[:, :T], dc == 0, dc == nchunk - 1),
                 r=[t.key, ones_b.key], w=[ps.key])
        ln = tmp()
        S.op("act", act(ln[:, :T], ps[:, :T], AF.Ln, bias=eps_t[:, 0:1], scale=scale),
             r=[ps.key, eps_t.key], w=[ln.key])
        rs = tmp()
        S.op("act", act(rs[:, :T], ln[:, :T], AF.Exp, scale=-0.5), r=[ln.key], w=[rs.key])
        return rs

    def norm_to_xn(T, vi):
        S.phase = 'norm'
        rs = rms_stats(T, lambda dc: x[:, dc, :T], xkeys, 8, 1.0 / D)
        for dc in range(8):
            S.op("dve", stt(xn[:, dc, :T], x[:, dc, :T], vcol(vi, dc), rs[:, :T], ALU.mult, ALU.mult),
                 r=[xkeys[dc], rs.key, VP.key], w=[xnkeys[dc]])

    def ffn(l, T, vi, w_gu, w_down):
        norm_to_xn(T, vi)
        S.phase = 'ffn_gu'
        gu = w_gu[l].rearrange("(dc p) (g f) -> p dc g f", p=128, g=2)
        for j in range(11):
            slot, sk = wload([gu[:, :, 0, j * 256:(j + 1) * 256], gu[:, :, 1, j * 256:(j + 1) * 256]], [128, 8, 2, 256])
            for sub in range(2):
                f = 2 * j + sub
                psg = bank()
                S.op("pe", [mm(psg[:, :T], slot[:, dc, 0, sub * 128:(sub + 1) * 128], xn[:, dc, :T], dc == 0, dc == 7)
                            for dc in range(8)], r=sk + xnkeys, w=[psg.key])
                psu = bank()
                S.op("pe", [mm(psu[:, :T], slot[:, dc, 1, sub * 128:(sub + 1) * 128], xn[:, dc, :T], dc == 0, dc == 7)
                            for dc in range(8)], r=sk + xnkeys, w=[psu.key])
                sg = tmp()
                S.op("act", act(sg[:, :T], psg[:, :T], AF.Silu), r=[psg.key], w=[sg.key])
                S.op("dve", tt(AR[:, f, :T], sg[:, :T], psu[:, :T], ALU.mult), r=[sg.key, psu.key], w=[ark(f)])
        S.phase = 'ffn_dn'
        dn = w_down[l].rearrange("(fc p) m -> p fc m", p=128)
        for j in range(8):
            slot, sk = wload(dn[:, :, j * 128:(j + 1) * 128], [128, NFC, 128])
            psd = bank()
            S.op("pe", [mm(psd[:, :T], slot[:, fc, :], AR[:, fc, :T], fc == 0, fc == NFC - 1) for fc in range(NFC)],
                 r=sk + [ark(f) for f in range(NFC)], w=[psd.key])
            S.op("dve", stt(x[:, j, :T], psd[:, :T], 0.5, x[:, j, :T], ALU.mult, ALU.add),
                 r=[psd.key, xkeys[j]], w=[xkeys[j]])

    def hist_of(l, seq):
        return histP[l] if seq == 0 else histS[seq - 1]

    def hst_of(l, seq):
        return hstP[l] if seq == 0 else hstS[seq - 1]

    def ST_of(l, seq):
        return ST_P[l] if seq == 0 else ST_S[seq - 1]

    def proj(slot, sk, sub, T, ps=None):
        if ps is None:
            ps = bank()
        S.op("pe", [mm(ps[:, :T], slot[:, dc, sub * 128:(sub + 1) * 128], xn[:, dc, :T], dc == 0, dc == 7)
                    for dc in range(8)], r=sk + xnkeys, w=[ps.key])
        return ps

    def conv_prep(l, cidx, ps, blk):
        st = stgs[stg_i[0] % 2]
        stg_i[0] += 1
        dg = dgs[dg_i[0] % 2]
        dg_i[0] += 1
        S.op("dve", tt(dg[:, :, :], ident_b[:, :].unsqueeze(1).to_broadcast([128, 4, 128]),
                       VP[:, cidx, l * 4:l * 4 + 4].unsqueeze(2).to_broadcast([128, 4, 128]), ALU.mult),
             r=[ident_b.key, VP.key], w=[dg.key])
        for (seq, c0, n, p0) in blk["runs"]:
            hist = hist_of(l, seq)
            hk = (hist.key, cidx)
            S.op("dve", cp(st[:, p0:p0 + 3], hist[:, cidx, :]), r=[hk], w=[st.key])
            S.op("act", act(st[:, p0 + 3:p0 + 3 + n], ps[:, c0:c0 + n], AF.Identity), r=[ps.key], w=[st.key])
            S.op("dve", cp(hist[:, cidx, :], ps[:, c0 + n - 3:c0 + n]), r=[ps.key, st.key], w=[hk])
        return st, dg

    def conv_mm(st, dg, blk, pc):
        for (seq, c0, n, p0) in blk["runs"]:
            S.op("pe", [mm(pc[:, c0:c0 + n], dg[:, j, :], st[:, p0 + j:p0 + j + n], j == 0, j == 3) for j in range(4)],
                 r=[dg.key, st.key], w=[pc.key])
        return pc

    def conv_chunk(l, cidx, ps, blk, pc=None):
        st, dg = conv_prep(l, cidx, ps, blk)
        if pc is None:
            pc = bank()
        return conv_mm(st, dg, blk, pc)

    def load_sample_state(l):
        PAx = PA.t[:, 0:128].rearrange("p (c r) -> p c r", r=4)
        for s in range(2):
            for g in range(4):
                if g < 3:
                    S.dma("sp", xtm_in[0:3, :], st_cq[l, s, :, g * 1024:(g + 1) * 1024], w=[xtm_in.key], skey="xtm_in")
                else:
                    S.dma("sp", xtm_in[0:3, :], st_cx[l, s, :, :], w=[xtm_in.key], skey="xtm_in")
                if g == 0:
                    S.dma("sp", xtm_in[3:4, :], st_h[l, s:s + 1, :], w=[xtm_in.key], skey="xtm_in")
                S.op("pe", [tr(PAx[:, 8 * g + c, 0:4], xtm_in[0:4, c * 128:(c + 1) * 128], ident_f[0:4, 0:4])
                            for c in range(8)], r=[xtm_in.key, ident_f.key], w=[PA.key])
            hk = [(histS[s].key, c) for c in range(32)]
            S.op("dve", cp(histS[s][:, :, :], PAx[:, :, 0:3]), r=[PA.key], w=hk)
            S.op("dve", cp(hstS[s][:, :], PAx[:, 0:8, 3]), r=[PA.key], w=[hstS[s].key])
            S.dma("sp", ST_S[s][:, :, :], st_S[l, s].rearrange("h p v -> p h v"), w=[ST_S[s].key], skey=ST_S[s].key)

    def write_tails(l, seq):
        hist = hist_of(l, seq)
        hst = hst_of(l, seq)
        ST = ST_of(l, seq)
        hk = [(hist.key, c) for c in range(32)]
        S.op("dve", cp(TT[:, 0:72].rearrange("p (r c) -> p r c", r=3), hist[:, 0:24, :].rearrange("p c r -> p r c")),
             r=hk, w=[TT.key])
        S.op("dve", cp(TT[:, 72:96].rearrange("p (r c) -> p r c", r=3), hist[:, 24:32, :].rearrange("p c r -> p r c")),
             r=hk, w=[TT.key])
        S.op("dve", cp(TT[:, 96:104], hst[:, :]), r=[hst.key], w=[TT.key])
        S.op("pe", tr(PA[0:104, 0:128], TT[:, 0:104], ident_f[:, :]), r=[TT.key, ident_f.key], w=[PA.key])
        S.op("act", act(TTt[0:104, :], PA[0:104, 0:128], AF.Identity), r=[PA.key], w=[TTt.key])
        if seq == 0:
            cq, cx, hh_, SS = o_pcq[l], o_pcx[l], o_ph[l:l + 1, :], o_pS[l]
        else:
            s = seq - 1
            cq, cx, hh_, SS = o_scq[l, s], o_scx[l, s], o_sh[l, s:s + 1, :], o_sS[l, s]
        for r_ in range(3):
            S.dma("sp", cq[r_, :].rearrange("(c f) -> c f", f=128), TTt[r_ * 24:(r_ + 1) * 24, :],
                  r=[TTt.key], skey="TTt")
            S.dma("sp", cx[r_, :].rearrange("(c f) -> c f", f=128), TTt[72 + r_ * 8:72 + (r_ + 1) * 8, :],
                  r=[TTt.key], skey="TTt")
        S.dma("sp", hh_.rearrange("o (c f) -> (o c) f", f=128), TTt[96:104, :], r=[TTt.key], skey="TTt")
        S.dma("sp", SS.rearrange("h p v -> p h v"), ST[:, :, :], r=[ST.key], skey=ST.key)

    def delta_pair(l, chs):
        n = len(chs)
        C = chs[0][2]
        R = C if n == 1 else 128
        nlev = int(math.log2(C))
        pos = [64 * i for i in range(n)]
        bta, al, g, gc, nb, egc, bge, edl, tcs = (scl[k_] for k_ in ("beta", "al", "g", "gc", "nb", "egc", "bge", "edl", "tc"))
        lo, hi = l * 8, l * 8 + 8
        mhi = [(triu.key, 'hi'), (maskL.key, 'hi'), (identX.key, 'hi')]

        def bc(ap, w_):
            return ap.unsqueeze(2).to_broadcast([R, 8, w_])

        def mmt(out, lhsT, rhs, tp, start=True, stop=True):
            return lambda e: e.matmul(out, lhsT=lhsT, rhs=rhs, start=start, stop=stop, tile_position=tp)

        def trt(out, in_, ident, tp):
            return lambda e: e.transpose(out, in_, ident, tile_position=tp)

        GB = [PB, PD]
        kks = [[ark(8 + h) for h in range(8)]]
        kk = [ark(8 + h) for h in range(8)]
        qk = [ark(h) for h in range(8)]
        vk = [ark(16 + h) for h in range(8)]
        fl = []
        for (seq, off, _), po in zip(chs, pos):
            fl += [mmt(PA[po:po + C, 0:16], xn[:, dc, off:off + C], wba[:, dc, :], (0, po), dc == 0, dc == 7) for dc in range(8)]
        S.op("pe", fl, r=xnkeys + [wba.key], w=[PA.key])
        S.op("act", act(bta[0:R, :], PA[0:R, 0:8], AF.Exp, scale=-1.0), r=[PA.key], w=[bta.key])
        S.op("act", act(bta[0:R, :], bta[0:R, :], AF.Ln, bias=one_t[0:R, 0:1]), r=[bta.key, one_t.key], w=[bta.key])
        S.op("act", act(bta[0:R, :], bta[0:R, :], AF.Exp, scale=-1.0), r=[bta.key], w=[bta.key])
        S.op("dve", tt(al[0:R, :], PA[0:R, 8:16], dtb[0:R, lo:hi], ALU.add), r=[PA.key, dtb.key], w=[al.key])
        S.op("act", act(al[0:R, :], al[0:R, :], AF.Exp), r=[al.key], w=[al.key])
        S.op("act", act(al[0:R, :], al[0:R, :], AF.Ln, bias=one_t[0:R, 0:1]), r=[al.key, one_t.key], w=[al.key])
        S.op("dve", tt(g[0:R, :], al[0:R, :], negA[0:R, lo:hi], ALU.mult), r=[al.key, negA.key], w=[g.key])
        S.op("dve", tt(Gtri[0:R, :, 0:C], triu[0:R, :, 0:C], bc(g[0:R, :], C), ALU.mult),
             r=[triu.key, g.key] + mhi, w=[Gtri.key])
        for i_, po in enumerate(pos):
            S.op("pe", mmt(pv(GB[i_], C), ones_f[po:po + C, :], Gtri[po:po + C, :, 0:C], (po, 0)),
                 r=[ones_f.key, Gtri.key], w=[GB[i_].key])
        S.op("pe", [mmt(PA[po:po + C, 16:24], triu[po:po + C, 0, 0:C], g[po:po + C, :], (po, po)) for po in pos],
             r=[triu.key, g.key] + mhi, w=[PA.key])
        S.op("act", act(gc[0:R, :], PA[0:R, 16:24], AF.Identity), r=[PA.key], w=[gc.key])
        for i_, po in enumerate(pos):
            gv = pv(GB[i_], C)
            S.op("act", act(EGs[i_][:, :, 0:C], gv, AF.Exp), r=[GB[i_].key], w=[EGs[i_].key])
            S.op("dve", tt(Dm[po:po + C, :, 0:C], gc[po:po + C, :].unsqueeze(2).to_broadcast([C, 8, C]), gv[po:po + C],
                           ALU.subtract), r=[gc.key, GB[i_].key], w=[Dm.key])
            S.op("dve", tt(tcs[po:po + C, :], gv[po:po + C, :, C - 1], gc[po:po + C, :], ALU.subtract),
                 r=[GB[i_].key, gc.key], w=[tcs.key])
        S.op("dve", ts(dL[0:R, :, 0:C], Dm[0:R, :, 0:C], 0.0, ALU.min), r=[Dm.key], w=[dL.key])
        S.op("act", act(dL[0:R, :, 0:C], dL[0:R, :, 0:C], AF.Exp), r=[dL.key], w=[dL.key])
        S.op("dve", tt(dL[0:R, :, 0:C], dL[0:R, :, 0:C], maskL[0:R, :, 0:C], ALU.mult), r=[dL.key, maskL.key] + mhi, w=[dL.key])
        S.op("dve", ts(dU[0:R, :, 0:C], Dm[0:R, :, 0:C], -1.0, ALU.mult, 0.0, ALU.min), r=[Dm.key], w=[dU.key])
        S.op("act", act(dU[0:R, :, 0:C], dU[0:R, :, 0:C], AF.Exp), r=[dU.key], w=[dU.key])
        S.op("dve", tt(dU[0:R, :, 0:C], dU[0:R, :, 0:C], maskU[0:R, :, 0:C], ALU.mult), r=[dU.key, maskU.key] + mhi, w=[dU.key])
        S.op("act", act(egc[0:R, :], gc[0:R, :], AF.Exp), r=[gc.key], w=[egc.key])
        S.op("dve", tt(bge[0:R, :], bta[0:R, :], egc[0:R, :], ALU.mult), r=[bta.key, egc.key], w=[bge.key])
        S.op("act", act(edl[0:R, :], tcs[0:R, :], AF.Exp), r=[tcs.key], w=[edl.key])
        S.op("dve", ts(nb[0:R, :], bta[0:R, :], -1.0, ALU.mult), r=[bta.key], w=[nb.key])
        PCv = pv(PC, C)
        fl = []
        for (seq, off, _), po in zip(chs, pos):
            fl += [mmt(PCv[po:po + C, h, :], AR[:, 8 + h, off:off + C], AR[:, 8 + h, off:off + C], (0, po)) for h in range(8)]
        S.op("pe", fl, r=kk, w=[PC.key])
        A0 = Am[0]
        S.op("dve", tt(Dm[0:R, :, 0:C], PCv[0:R], bc(nb[0:R, :], C), ALU.mult), r=[PC.key, nb.key, Dm.key], w=[Dm.key])
        S.op("dve", tt(A0[0:R, :, 0:C], Dm[0:R, :, 0:C], dL[0:R, :, 0:C], ALU.mult), r=[Dm.key, dL.key], w=[A0.key])
        PAb = PA.t[:, :].bitcast(BF16)[:, 0:8 * C].rearrange("p (h c) -> p h c", c=C)
        fl = []
        for po in pos:
            fl += [trt(PAb[po:po + C, h, :], A0[po:po + C, h, 0:C], ident_b[po:po + C, po:po + C], (po, po)) for h in range(8)]
        S.op("pe", fl, r=[A0.key, ident_b.key], w=[PA.key])
        S.op("act", act(UX[0][0:R, :, 0:C], PAb[0:R], AF.Identity), r=[PA.key], w=[UX[0].key])
        S.op("dve", cp(UX[0][0:R, :, C:2 * C], identX[0:R, :, 0:C]), r=[identX.key, UX[0].key] + mhi, w=[UX[0].key])
        PEv = pv(PE2, C, 2 * C)

        def gen_neumann():
            cur = 0
            for k in range(nlev):
                Ac, UXc, An, UXn = Am[cur], UX[cur], Am[1 - cur], UX[1 - cur]
                if k < nlev - 1:
                    fl = []
                    for po in pos:
                        fl += [mmt(PEv[po:po + C, h, :], Ac[po:po + C, h, 0:C], UXc[po:po + C, h, 0:2 * C], (po, po)) for h in range(8)]
                    S.op("pe", fl, r=[Ac.key, UXc.key], w=[PE2.key]); yield
                    fl = []
                    for po in pos:
                        fl += [mmt(PCv[po:po + C, h, :], UXc[po:po + C, h, 0:C], Ac[po:po + C, h, 0:C], (po, po)) for h in range(8)]
                    S.op("pe", fl, r=[Ac.key, UXc.key], w=[PC.key]); yield
                    S.op("act", act(UXn[0:R, :, 0:C], PEv[0:R, :, 0:C], AF.Identity), r=[PE2.key], w=[UXn.key]); yield
                    S.op("dve", tt(UXn[0:R, :, C:2 * C], PEv[0:R, :, C:2 * C], UXc[0:R, :, C:2 * C], ALU.add),
                         r=[PE2.key, UXc.key, UXn.key], w=[UXn.key]); yield
                    S.op("act", act(An[0:R, :, 0:C], PCv[0:R], AF.Identity), r=[PC.key], w=[An.key]); yield
                else:
                    fl = []
                    for po in pos:
                        fl += [mmt(PEv[po:po + C, h, C:2 * C], Ac[po:po + C, h, 0:C], UXc[po:po + C, h, C:2 * C], (po, po)) for h in range(8)]
                    S.op("pe", fl, r=[Ac.key, UXc.key], w=[PE2.key]); yield
                    S.op("dve", tt(Xb[0:R, :, 0:C], PEv[0:R, :, C:2 * C], UXc[0:R, :, C:2 * C], ALU.add),
                         r=[PE2.key, UXc.key], w=[Xb.key]); yield
                cur = 1 - cur

        PFb = PF2.t[:, :].bitcast(BF16).rearrange("p (h d) -> p h d", d=128)
        PDq = pv(PD, C)

        def gen_other():
            fl = []
            for (seq, off, _), po in zip(chs, pos):
                fl += [trt(PFb[po:po + C, h, :], AR[:, 8 + h, off:off + C], ident_b[:, :], (0, po)) for h in range(8)]
                fl += [trt(PFb[po:po + C, 8 + h, :], AR[:, 16 + h, off:off + C], ident_b[:, :], (0, po)) for h in range(8)]
            S.op("pe", fl, r=kk + vk + [ident_b.key], w=[PF2.key]); yield
            S.op("dve", tt(vb[0:R, :, :], PFb[0:R, 8:16, :], bc(bta[0:R, :], 128), ALU.mult), r=[PF2.key, bta.key], w=[vb.key]); yield
            S.op("dve", tt(kbg[0:R, :, :], PFb[0:R, 0:8, :], bc(bge[0:R, :], 128), ALU.mult), r=[PF2.key, bge.key], w=[kbg.key]); yield
            S.op("dve", tt(kd[0:R, :, :], PFb[0:R, 0:8, :], bc(edl[0:R, :], 128), ALU.mult), r=[PF2.key, edl.key], w=[kd.key]); yield
            fl = []
            for (seq, off, _), po in zip(chs, pos):
                fl += [mmt(PDq[po:po + C, h, :], AR[:, 8 + h, off:off + C], AR[:, h, off:off + C], (0, po)) for h in range(8)]
            S.op("pe", fl, r=kk + qk, w=[PD.key]); yield
            S.op("dve", tt(aT[0:R, :, 0:C], PDq[0:R], dU[0:R, :, 0:C], ALU.mult), r=[PD.key, dU.key], w=[aT.key]); yield
            for i_, ((seq, off, _), po) in enumerate(zip(chs, pos)):
                S.op("dve", tt(qds[i_][:, :, 0:C], AR[:, 0:8, off:off + C], EGs[i_][:, :, 0:C], ALU.mult),
                     r=qk + [EGs[i_].key], w=[qds[i_].key]); yield

        gens = [gen_neumann(), gen_other()]
        while gens:
            for g_ in list(gens):
                try:
                    next(g_)
                except StopIteration:
                    gens.remove(g_)
        WB = [PA, PB]
        for i_, po in enumerate(pos):
            wv = pv(WB[i_], C)
            S.op("pe", [mmt(wv[:, h, :], kbg[po:po + C, h, :], Xb[po:po + C, h, 0:C], (po, 0)) for h in range(8)],
                 r=[kbg.key, Xb.key], w=[WB[i_].key])
            S.op("act", act(wTns[i_][:, :, 0:C], wv, AF.Identity, scale=-1.0), r=[WB[i_].key], w=[wTns[i_].key])
        PEw = PE2.t[:, :].rearrange("p (h d) -> p h d", d=128)
        PFs = PF2.t[:, :].rearrange("p (h d) -> p h d", d=128)
        PBv = pv(PB, C)
        for i_, ((seq, off, _), po) in enumerate(zip(chs, pos)):
            ST = ST_of(l, seq)
            SBs = SBb[seq]
            wTn, qd, EG = wTns[i_], qds[i_], EGs[i_]
            fl = []
            for h in range(8):
                fl.append(mmt(PEw[po:po + C, h, :], Xb[po:po + C, h, 0:C], vb[po:po + C, h, :], (po, po), True, False))
                fl.append(mmt(PEw[po:po + C, h, :], wTn[:, h, 0:C], SBs[:, h, :], (0, po), False, True))
            S.op("pe", fl, r=[Xb.key, vb.key, wTn.key, SBs.key], w=[PE2.key])
            S.op("act", act(vnb[po:po + C, :, :], PEw[po:po + C], AF.Identity), r=[PE2.key], w=[vnb.key])
            fl = []
            for h in range(8):
                fl.append(mmt(PBv[:, h, :], SBs[:, h, :], qd[:, h, 0:C], (0, 0), True, False))
                fl.append(mmt(PBv[:, h, :], vnb[po:po + C, h, :], aT[po:po + C, h, 0:C], (po, 0), False, True))
            S.op("pe", fl, r=[SBs.key, qd.key, vnb.key, aT.key], w=[PB.key])
            S.op("act", act(ofm[:, :, off:off + C], PBv, AF.Identity), r=[PB.key], w=[ofm.key])
            S.op("pe", [mmt(PFs[:, h, :], kd[po:po + C, h, :], vnb[po:po + C, h, :], (po, 0)) for h in range(8)],
                 r=[kd.key, vnb.key], w=[PF2.key])
            S.op("dve", tt(ST[:, :, :], ST[:, :, :], EG[:, :, C - 1:C].to_broadcast([128, 8, 128]), ALU.mult),
                 r=[ST.key, EG.key], w=[ST.key])
            S.op("dve", tt(ST[:, :, :], ST[:, :, :], PFs, ALU.add), r=[ST.key, PF2.key], w=[ST.key])
            S.op("act", act(SBs[:, :, :], ST[:, :, :], AF.Identity), r=[ST.key], w=[SBs.key])

    def mixer(l, blk):
        T = blk["T"]
        vi_mix = l * 7 + 1
        norm_to_xn(T, vi_mix)
        win = w_in[l].rearrange("(dc p) c -> p dc c", p=128)
        if blk["last"]:
            load_sample_state(l)
        S.phase = 'qkv'
        pendM = None
        pendS = None
        PSB = [PA, PB]
        PCB = [PC, PD]

        def p1_flush_silu():
            S.op("act", act(AR[:, pendS[0], :T], pendS[1][:, :T], AF.Silu), r=[pendS[1].key], w=[ark(pendS[0])])

        for tile_i in range(6):
            slot, sk = wload(win[:, :, tile_i * 512:(tile_i + 1) * 512], [128, 8, 512])
            for sub in range(4):
                cidx = tile_i * 4 + sub
                ps = proj(slot, sk, sub, T, PSB[cidx % 2])
                st, dg = conv_prep(l, cidx, ps, blk)
                if pendM is not None:
                    pc_ = conv_mm(pendM[1], pendM[2], blk, PCB[pendM[0] % 2])
                    if pendS is not None:
                        p1_flush_silu()
                    pendS = (pendM[0], pc_)
                pendM = (cidx, st, dg)
        pc_ = conv_mm(pendM[1], pendM[2], blk, PCB[pendM[0] % 2])
        p1_flush_silu()
        pendS = (pendM[0], pc_)
        p1_flush_silu()

        def qk_Y(ctx):
            cidx, psn = ctx
            ln = tmp()
            S.op("act", act(ln[:, :T], psn[:, :T], AF.Ln, bias=eps_t[:, 0:1]), r=[psn.key, eps_t.key], w=[ln.key])
            if cidx < 8:
                S.op("act", act(ln[:, :T], ln[:, :T], AF.Exp, scale=-0.5, bias=lnq_t[:, 0:1]),
                     r=[ln.key, lnq_t.key], w=[ln.key])
            else:
                S.op("act", act(ln[:, :T], ln[:, :T], AF.Exp, scale=-0.5), r=[ln.key], w=[ln.key])
            S.op("dve", tt(AR[:, cidx, :T], AR[:, cidx, :T], ln[:, :T], ALU.mult), r=[ark(cidx), ln.key], w=[ark(cidx)])

        pendY = None
        for cidx in range(16):
            sq = tmp()
            S.op("dve", tt(bfv(sq)[:, :T], AR[:, cidx, :T], AR[:, cidx, :T], ALU.mult), r=[ark(cidx)], w=[sq.key])
            psn = bank()
            S.op("pe", mm(psn[:, :T], ones_b[:, :], bfv(sq)[:, :T]), r=[sq.key, ones_b.key], w=[psn.key])
            if pendY is not None:
                qk_Y(pendY)
            pendY = (cidx, psn)
        qk_Y(pendY)
        stage('qkv')
        slot, sk = wload(win[:, :, OFF_BA:OFF_BA + 16], [128, 8, 16])
        S.op("act", act(wba[:, :, :], slot, AF.Identity), r=sk, w=[wba.key])
        S.phase = 'delta'
        started = set()
        segs = list(blk["segs"])
        groups = []
        i_ = 0
        while i_ < len(segs):
            if (i_ + 1 < len(segs) and segs[i_][2] == 64 and segs[i_ + 1][2] == 64 and segs[i_][0] == segs[i_ + 1][0]):
                groups.append([segs[i_], segs[i_ + 1]])
                i_ += 2
            else:
                groups.append([segs[i_]])
                i_ += 1
        for grp in groups:
            for (seq, off, C) in grp:
                if seq not in started:
                    started.add(seq)
                    ST = ST_of(l, seq)
                    S.op("act", act(SBb[seq][:, :, :], ST[:, :, :], AF.Identity), r=[ST.key], w=[SBb[seq].key])
            delta_pair(l, grp)
        stage('delta')
        S.phase = 'gnorm'
        def gn_B(ctx):
            h, zs, sq = ctx
            psn = bank()
            S.op("pe", mm(psn[:, :T], ones_b[:, :], bfv(sq)[:, :T]), r=[sq.key, ones_b.key], w=[psn.key]); yield
            ln = tmp()
            S.op("act", act(ln[:, :T], psn[:, :T], AF.Ln, bias=eps_t[:, 0:1], scale=1.0 / 128),
                 r=[psn.key, eps_t.key], w=[ln.key]); yield
            rs = ln
            S.op("act", act(rs[:, :T], ln[:, :T], AF.Exp, scale=-0.5), r=[ln.key], w=[rs.key]); yield
            t = tmp()
            S.op("dve", stt(t[:, :T], ofm[:, h, :T], onw(l), rs[:, :T], ALU.mult, ALU.mult),
                 r=[ofm.key, rs.key, VP.key], w=[t.key]); yield
            S.op("dve", tt(AR[:, 8 + h, :T], t[:, :T], zs[:, :T], ALU.mult), r=[t.key, zs.key], w=[ark(8 + h)]); yield

        def gn_A(h, slot, sk, sub, out):
            psz = proj(slot, sk, sub, T); yield
            zs = tmp()
            S.op("act", act(zs[:, :T], psz[:, :T], AF.Silu), r=[psz.key], w=[zs.key]); yield
            sq = tmp()
            S.op("act", act(bfv(sq)[:, :T], ofm[:, h, :T], AF.Square), r=[ofm.key], w=[sq.key]); yield
            out.append((h, zs, sq))

        def gn_interleave(ga, gb):
            live = [g_ for g_ in (ga, gb) if g_ is not None]
            while live:
                for g_ in list(live):
                    try:
                        next(g_)
                    except StopIteration:
                        live.remove(g_)

        pend = None
        for tile_i in range(2):
            slot, sk = wload(win[:, :, OFF_Z + tile_i * 512:OFF_Z + (tile_i + 1) * 512], [128, 8, 512])
            for sub in range(4):
                h = tile_i * 4 + sub
                out = []
                gn_interleave(gn_A(h, slot, sk, sub, out), gn_B(pend) if pend is not None else None)
                pend = out[0]
        gn_interleave(gn_B(pend), None)
        stage('gnorm')
        S.phase = 'lru'
        slot, sk = wload(lru_w_r[l].rearrange("n c d -> c n d"), [128, 8, 128])
        S.op("act", act(wri[:, 0:8, :], slot, AF.Identity), r=sk, w=[wri.key])
        slot, sk = wload(lru_w_i[l].rearrange("n c d -> c n d"), [128, 8, 128])
        S.op("act", act(wri[:, 8:16, :], slot, AF.Identity), r=sk, w=[wri.key])
        vb_ = l * 7 + 3
        PSR = Buf(PE2.t[:, 0:512], PE2.key)
        PSI = Buf(PF2.t[:, 0:512], PF2.key)
        PCB = [PC, PD]

        def lru_B(ctx):
            n_, xc, xcb, xl = ctx
            psr = PSR
            S.op("pe", mm(psr[:, :T], wri[:, n_, :], xcb[:, :T]), r=[wri.key, xcb.key], w=[psr.key]); yield
            rr = tmp()
            S.op("act", act(rr[:, :T], psr[:, :T], AF.Sigmoid, bias=vcol(vb_ + 1, n_)), r=[psr.key, VP.key], w=[rr.key]); yield
            psi = PSI
            S.op("pe", mm(psi[:, :T], wri[:, 8 + n_, :], xcb[:, :T]), r=[wri.key, xcb.key], w=[psi.key]); yield
            ii = tmp()
            S.op("act", act(ii[:, :T], psi[:, :T], AF.Sigmoid, bias=vcol(vb_ + 2, n_)), r=[psi.key, VP.key], w=[ii.key]); yield
            S.op("dve", tt(ii[:, :T], ii[:, :T], xc[:, :T], ALU.mult), r=[ii.key, xc.key], w=[ii.key]); yield
            aa = rr
            S.op("act", act(aa[:, :T], rr[:, :T], AF.Exp, scale=c8[:, l * 8 + n_:l * 8 + n_ + 1]),
                 r=[rr.key, c8.key], w=[aa.key]); yield
            a2 = tmp()
            S.op("dve", tt(a2[:, :T], aa[:, :T], aa[:, :T], ALU.mult), r=[aa.key], w=[a2.key]); yield
            S.op("act", act(a2[:, :T], a2[:, :T], AF.Sqrt, bias=one_t[:, 0:1], scale=-1.0), r=[a2.key, one_t.key], w=[a2.key]); yield
            S.op("dve", tt(ii[:, :T], ii[:, :T], a2[:, :T], ALU.mult), r=[ii.key, a2.key], w=[ii.key]); yield
            hh = hhs[hh_i[0] % 2]
            hh_i[0] += 1
            for (seq, c0, n, p0) in blk["runs"]:
                hst = hst_of(l, seq)
                S.op("dve", lambda e, c0=c0, n=n, hst=hst: e.tensor_tensor_scan(
                    out=hh[:, c0:c0 + n], data0=aa[:, c0:c0 + n], data1=ii[:, c0:c0 + n],
                    initial=hst[:, n_:n_ + 1], op0=ALU.mult, op1=ALU.add),
                    r=[aa.key, ii.key, hst.key], w=[hh.key]); yield
                S.op("dve", cp(hst[:, n_:n_ + 1], hh[:, c0 + n - 1:c0 + n]), r=[hh.key], w=[hst.key])
            S.op("dve", tt(AR[:, 16 + n_, :T], hh[:, :T], xl[:, :T], ALU.mult), r=[hh.key, xl.key], w=[ark(16 + n_)]); yield

        def lru_A(n_, slx, kx, sly, ky, sub, out):
            ps = proj(slx, kx, sub, T, PA); yield
            st, dg = conv_prep(l, 24 + n_, ps, blk); yield
            psy = proj(sly, ky, sub, T, PB); yield
            pc_ = conv_mm(st, dg, blk, PCB[n_ % 2]); yield
            xl = tmp()
            S.op("act", act(xl[:, :T], psy[:, :T], AF.Identity), r=[psy.key], w=[xl.key]); yield
            x2 = tmp()
            S.op("act", act(x2[:, :T], psy[:, :T], AF.Square), r=[psy.key], w=[x2.key]); yield
            xc = tmp()
            S.op("act", act(xc[:, :T], pc_[:, :T], AF.Identity, bias=vcol(vb_, n_)), r=[pc_.key, VP.key], w=[xc.key]); yield
            S.op("dve", ts(x2[:, :T], x2[:, :T], 0.044715, ALU.mult, 1.0, ALU.add), r=[x2.key], w=[x2.key]); yield
            xcb = tmpbf()
            S.op("dve", cp(xcb[:, :T], xc[:, :T]), r=[xc.key], w=[xcb.key]); yield
            S.op("dve", tt(x2[:, :T], x2[:, :T], xl[:, :T], ALU.mult), r=[x2.key, xl.key], w=[x2.key]); yield
            S.op("act", act(x2[:, :T], x2[:, :T], AF.Sigmoid, scale=1.5957691216057308), r=[x2.key], w=[x2.key]); yield
            S.op("dve", tt(xl[:, :T], xl[:, :T], x2[:, :T], ALU.mult), r=[xl.key, x2.key], w=[xl.key]); yield
            out.append((n_, xc, xcb, xl))

        def interleave(ga, gb):
            live = [g_ for g_ in (ga, gb) if g_ is not None]
            while live:
                for g_ in list(live):
                    try:
                        next(g_)
                    except StopIteration:
                        live.remove(g_)

        pend = None
        for tile_i in range(2):
            slx, kx = wload(win[:, :, OFF_LX + tile_i * 512:OFF_LX + (tile_i + 1) * 512], [128, 8, 512])
            sly, ky = wload(win[:, :, OFF_LY + tile_i * 512:OFF_LY + (tile_i + 1) * 512], [128, 8, 512])
            for sub in range(4):
                n_ = tile_i * 4 + sub
                out = []
                interleave(lru_A(n_, slx, kx, sly, ky, sub, out), lru_B(pend) if pend is not None else None)
                pend = out[0]
        interleave(lru_B(pend), None)
        stage('lru')
        S.phase = 'merge'
        wa = w_branch_a[l].rearrange("(dc p) c -> p dc c", p=128)
        wb_ = w_branch_b[l].rearrange("(dc p) c -> p dc c", p=128)
        for tile_i in range(2):
            sa, ka = wload(wa[:, :, tile_i * 512:(tile_i + 1) * 512], [128, 8, 512])
            sga_, kga = wload(win[:, :, OFF_GA + tile_i * 512:OFF_GA + (tile_i + 1) * 512], [128, 8, 512])
            for sub in range(4):
                m = tile_i * 4 + sub
                psa = bank()
                S.op("pe", [mm(psa[:, :T], sa[:, h, sub * 128:(sub + 1) * 128], AR[:, 8 + h, :T], h == 0, h == 7)
                            for h in range(8)], r=ka + [ark(8 + h) for h in range(8)], w=[psa.key])
                psg = proj(sga_, kga, sub, T)
                sg = tmp()
                S.op("act", act(sg[:, :T], psg[:, :T], AF.Sigmoid), r=[psg.key], w=[sg.key])
                S.op("dve", tt(ofm[:, m, :T], sg[:, :T], psa[:, :T], ALU.mult), r=[sg.key, psa.key], w=[ofm.key])
            sb_, kb = wload(wb_[:, :, tile_i * 512:(tile_i + 1) * 512], [128, 8, 512])
            sgb_, kgb = wload(win[:, :, OFF_GB + tile_i * 512:OFF_GB + (tile_i + 1) * 512], [128, 8, 512])
            for sub in range(4):
                m = tile_i * 4 + sub
                psb = bank()
                S.op("pe", [mm(psb[:, :T], sb_[:, h, sub * 128:(sub + 1) * 128], AR[:, 16 + h, :T], h == 0, h == 7)
                            for h in range(8)], r=kb + [ark(16 + h) for h in range(8)], w=[psb.key])
                psg = proj(sgb_, kgb, sub, T)
                sg = tmp()
                S.op("act", act(sg[:, :T], psg[:, :T], AF.Sigmoid), r=[psg.key], w=[sg.key])
                t = tmp()
                S.op("dve", tt(t[:, :T], sg[:, :T], psb[:, :T], ALU.mult), r=[sg.key, psb.key], w=[t.key])
                S.op("dve", tt(AR[:, m, :T], t[:, :T], ofm[:, m, :T], ALU.add), r=[t.key, ofm.key], w=[ark(m)])
        stage('merge')
        S.phase = 'wout'
        wo = w_out[l].rearrange("(dc p) c -> p dc c", p=128)
        for tile_i in range(2):
            so, ko = wload(wo[:, :, tile_i * 512:(tile_i + 1) * 512], [128, 8, 512])
            for sub in range(4):
                m = tile_i * 4 + sub
                pso = bank()
                S.op("pe", [mm(pso[:, :T], so[:, h, sub * 128:(sub + 1) * 128], AR[:, h, :T], h == 0, h == 7)
                            for h in range(8)], r=ko + [ark(h) for h in range(8)], w=[pso.key])
                S.op("dve", tt(x[:, m, :T], x[:, m, :T], pso[:, :T], ALU.add), r=[xkeys[m], pso.key], w=[xkeys[m]])
        S.phase = 'tails'
        if blk["last"]:
            for seq in (0, 1, 2):
                write_tails(l, seq)

    blocks = make_blocks()
    srcmap = {"meta": meta, "xp": xp, "xs": xs}
    try:
      stage('const')
      for b in range(5):
        if blocks_sel is not None and b not in blocks_sel:
            continue
        blk = blocks[b]
        T = blk["T"]
        wl_blk[0] = 0
        wl_pass[0] = 0 if nblk_done[0] == 0 else 1
        nblk_done[0] += 1
        S.phase = 'load'
        rows = []
        for (nm, r0, n) in blk["srcs"]:
            rows += [(nm, r0 + i) for i in range(n)]
        ntile = (T + 127) // 128
        for ti in range(ntile):
            c0 = ti * 128
            n = min(128, T - c0)
            i = 0
            while i < n:
                nm, r0 = rows[c0 + i]
                j = i
                while j + 1 < n and rows[c0 + j + 1] == (nm, r0 + (j + 1 - i)):
                    j += 1
                cnt = j - i + 1
                S.dma("sp", xtm_in[i:i + cnt, :], srcmap[nm][r0:r0 + cnt, :], w=[xtm_in.key], skey="xtm_in")
                i = j + 1
            for half in range(2):
                Pt = PE2 if half == 0 else PF2
                S.op("pe", [tr(Pt[:, q * 128:q * 128 + n], xtm_in[0:n, (half * 4 + q) * 128:(half * 4 + q + 1) * 128],
                               ident_f[0:n, 0:n]) for q in range(4)], r=[xtm_in.key, ident_f.key], w=[Pt.key])
                for q in range(4):
                    dc = half * 4 + q
                    S.op("act" if q % 2 else "dve",
                         (act(x[:, dc, c0:c0 + n], Pt[:, q * 128:q * 128 + n], AF.Identity) if q % 2 else
                          cp(x[:, dc, c0:c0 + n], Pt[:, q * 128:q * 128 + n])),
                         r=[Pt.key], w=[xkeys[dc]])
        stage('load')
        for l in range(nl):
            ffn(l, T, l * 7 + 0, ffn1_w_gu, ffn1_w_down)
            stage('ffn1')
            mixer(l, blk)
            stage('mixer')
            ffn(l, T, l * 7 + 2, ffn2_w_gu, ffn2_w_down)
            stage('ffn2')
        S.phase = 'final'
        rs = rms_stats(T, lambda dc: x[:, dc, :T], xkeys, 8, 1.0 / D)
        for dc in range(8):
            S.op("dve", stt(ofm[:, dc, :T], x[:, dc, :T], vcol(28, dc), rs[:, :T], ALU.mult, ALU.mult),
                 r=[xkeys[dc], rs.key, VP.key], w=[ofm.key])
        dst = []
        if b == 0:
            dst += [None] * 16 + [("y_p", i) for i in range(T - 16)]
        elif b < 4:
            dst += [("y_p", blk["p0"] - 16 + i) for i in range(T)]
        else:
            dst += [("y_p", blk["p0"] - 16 + i) for i in range(T - 64)] + [("y_s", i) for i in range(64)]
        dmap = {"y_p": y_p, "y_s": y_s}
        for ti in range(ntile):
            c0 = ti * 128
            n = min(128, T - c0)
            for half in range(2):
                Pt = PE2 if half == 0 else PF2
                S.op("pe", [tr(Pt[0:n, q * 128:(q + 1) * 128], ofm[:, half * 4 + q, c0:c0 + n], ident_f[:, :])
                            for q in range(4)], r=[ofm.key, ident_f.key], w=[Pt.key])
                S.op("act" if half else "dve",
                     (act(xtm_out[0:n, half * 512:(half + 1) * 512], Pt[0:n, 0:512], AF.Identity) if half else
                      cp(xtm_out[0:n, half * 512:(half + 1) * 512], Pt[0:n, 0:512])),
                     r=[Pt.key], w=[xtm_out.key])
            i = 0
            while i < n:
                if dst[c0 + i] is None:
                    i += 1
                    continue
                nm, r0 = dst[c0 + i]
                j = i
                while j + 1 < n and dst[c0 + j + 1] == (nm, r0 + (j + 1 - i)):
                    j += 1
                cnt = j - i + 1
                S.dma("sp", dmap[nm][r0:r0 + cnt, :], xtm_out[i:i + cnt, :], r=[xtm_out.key], skey="xtm_out")
                i = j + 1
    except _Stop:
        pass
    S.finish()
    return nc, es, S


_CACHE = {}


def kernel(**inputs):
    nl = NL_FULL
    if "prog" not in _CACHE:
        _CACHE["prog"] = build(nl)
    nc, es, S = _CACHE["prog"]
    f = lambda a: np.ascontiguousarray(np.asarray(a, dtype=np.float32))
    wnames = ["ffn1_norm", "ffn1_w_gu", "ffn1_w_down", "mix_norm", "w_in", "delta_conv_w", "delta_A_log",
              "delta_dt_bias", "delta_out_norm", "lru_conv_w", "lru_conv_b", "lru_w_r", "lru_b_r", "lru_w_i",
              "lru_b_i", "lru_lambda", "w_branch_a", "w_branch_b", "w_out", "ffn2_norm", "ffn2_w_gu", "ffn2_w_down"]
    shared = {n: f(inputs[n]) for n in wnames}
    shared["meta"] = f(inputs["meta_tokens"])
    shared["final_norm"] = f(inputs["final_norm"]).reshape(1, D)
    x_prompt = f(inputs["x_prompt"])
    x_sample = f(inputs["x_sample"])
    sS = f(inputs["state_delta_S"])
    scq = f(inputs["state_delta_conv"])
    sh = f(inputs["state_lru_h"])
    scx = f(inputs["state_lru_conv"])
    in_maps = []
    for c in range(8):
        m = dict(shared)
        m["xp"] = x_prompt[c]
        m["xs"] = np.ascontiguousarray(x_sample[2 * c:2 * c + 2].reshape(64, D))
        m["st_S"] = np.ascontiguousarray(sS[:, 2 * c:2 * c + 2])
        m["st_cq"] = np.ascontiguousarray(scq[:, 2 * c:2 * c + 2])
        m["st_h"] = np.ascontiguousarray(sh[:, 2 * c:2 * c + 2])
        m["st_cx"] = np.ascontiguousarray(scx[:, 2 * c:2 * c + 2])
        in_maps.append(m)
    res = run_bass_kernel_spmd(nc, in_maps, core_ids=list(range(8)))
    R = res.results
    y_prompt = np.stack([R[c]["y_p"] for c in range(8)], 0)
    y_sample = np.concatenate([R[c]["y_s"].reshape(2, 32, D) for c in range(8)], 0)
    p_S = np.stack([R[c]["o_pS"] for c in range(8)], 1)
    p_cq = np.stack([R[c]["o_pcq"] for c in range(8)], 1)
    p_h = np.stack([R[c]["o_ph"] for c in range(8)], 1)
    p_cx = np.stack([R[c]["o_pcx"] for c in range(8)], 1)
    s_S = np.concatenate([R[c]["o_sS"] for c in range(8)], 1)
    s_cq = np.concatenate([R[c]["o_scq"] for c in range(8)], 1)
    s_h = np.concatenate([R[c]["o_sh"] for c in range(8)], 1)
    s_cx = np.concatenate([R[c]["o_scx"] for c in range(8)], 1)
    outs = (y_prompt, y_sample, p_S, p_cq, p_h, p_cx, s_S, s_cq, s_h, s_cx)
    return tuple(np.ascontiguousarray(o.astype(np.float32)) for o in outs)
```
